# Optimizing a Trainium2 kernel written in Bass

```python
import jax, jax.numpy as jnp
from jax import lax
import numpy as np

D_MODEL = 2048
BATCH = 2
SEQ = 16384
DEPTH = 2

EPS = 1e-6
NEG = -1e30
ROPE_THETA = 10000.0
ATT_HEADS = 4
ATT_HEAD_DIM = 128
ATT_WIDTH = ATT_HEADS * ATT_HEAD_DIM
DILATED_PATTERNS = ((128, 1), (512, 4), (2048, 16))
MLSTM_HEADS = 6
MLSTM_QK_DIM = 64
MLSTM_V_DIM = 128
MLSTM_QK_WIDTH = MLSTM_HEADS * MLSTM_QK_DIM
MLSTM_V_WIDTH = MLSTM_HEADS * MLSTM_V_DIM
MLSTM_CHUNK = 64
CONV_WIDTH = 5
GLA_HEADS = 6
GLA_K_DIM = 64
GLA_V_DIM = 128
GLA_K_WIDTH = GLA_HEADS * GLA_K_DIM
GLA_V_WIDTH = GLA_HEADS * GLA_V_DIM
GLA_RANK = 16
GLA_TAU = 16.0
GLA_CHUNK = 64
MIX_WIDTH = ATT_WIDTH + MLSTM_V_WIDTH + GLA_V_WIDTH
MEM_LEN = 256
XATT_HEADS = 4
XATT_HEAD_DIM = 128
XATT_WIDTH = XATT_HEADS * XATT_HEAD_DIM
D_FF = 4 * D_MODEL
IN_SPLITS = (ATT_WIDTH, ATT_WIDTH, ATT_WIDTH,
             MLSTM_QK_WIDTH, MLSTM_QK_WIDTH, MLSTM_V_WIDTH, MLSTM_V_WIDTH, 2 * MLSTM_HEADS, 2 * MLSTM_HEADS,
             GLA_K_WIDTH, GLA_K_WIDTH, GLA_V_WIDTH, GLA_V_WIDTH, 2 * GLA_RANK)
N_IN = sum(IN_SPLITS)

kernel_name = 'hybrid_dilated_mlstm_gla_encoder'


def rms_norm(x, g):
    xf = x.astype(jnp.float32)
    y = xf * lax.rsqrt(jnp.mean(xf * xf, axis=-1, keepdims=True) + EPS)
    return (y * g.astype(jnp.float32)).astype(x.dtype)


def head_norm(t, g, dtype):
    t = t.astype(jnp.float32)
    y = t * lax.rsqrt(jnp.mean(t * t, axis=-1, keepdims=True) + EPS)
    y = y.reshape(t.shape[0], t.shape[1], -1)
    return (y * g.astype(jnp.float32)).astype(dtype)


def rotary(t, pos):
    half = t.shape[-1] // 2
    inv = ROPE_THETA ** (-jnp.arange(half, dtype=jnp.float32) / half)
    ang = pos.astype(jnp.float32)[:, None] * inv[None, :]
    cos = jnp.cos(ang)[:, None, :]
    sin = jnp.sin(ang)[:, None, :]
    tf = t.astype(jnp.float32)
    t1, t2 = tf[..., :half], tf[..., half:]
    return jnp.concatenate([t1 * cos - t2 * sin, t2 * cos + t1 * sin], axis=-1).astype(t.dtype)


def centred_depthwise_conv(x, w):
    k = w.shape[0]
    return lax.conv_general_dilated(x, w[:, None, :].astype(x.dtype), window_strides=(1,),
                                    padding=[(k // 2, k // 2)],
                                    dimension_numbers=('NWC', 'WIO', 'NWC'),
                                    feature_group_count=x.shape[-1])


def to_chunks(t, size):
    b, s, h = t.shape[:3]
    t = t.reshape((b, s // size, size, h) + t.shape[3:])
    return t.transpose((1, 0, 3, 2) + tuple(range(4, t.ndim)))


def from_chunks(t):
    nc, b, h, l = t.shape[:4]
    t = t.transpose((1, 0, 3, 2) + tuple(range(4, t.ndim)))
    return t.reshape((b, nc * l, h) + t.shape[4:])


def dilated_window_attention(q, k, v, window, dilation):
    b, s, h, e = q.shape
    w = window // (2 * dilation)
    n = s // dilation
    nb = -(-n // w)
    lp = nb * w

    def strided(t):
        t = t.reshape(b, n, dilation, h, e)
        return jnp.pad(t, ((0, 0), (0, lp - n), (0, 0), (0, 0), (0, 0)))

    def windows(t):
        t = jnp.pad(strided(t), ((0, 0), (w, w), (0, 0), (0, 0), (0, 0))).reshape(b, nb + 2, w, dilation, h, e)
        return jnp.concatenate([t[:, :-2], t[:, 1:-1], t[:, 2:]], axis=2)

    qs = strided(q).reshape(b, nb, w, dilation, h, e)
    kw, vw = windows(k), windows(v)
    qi = jnp.arange(nb)[:, None, None] * w + jnp.arange(w)[None, :, None]
    ki = (jnp.arange(nb)[:, None, None] - 1) * w + jnp.arange(3 * w)[None, None, :]
    mask = (jnp.abs(ki - qi) <= w) & (ki >= 0) & (ki < n)
    sc = jnp.einsum('bnqdhe,bnkdhe->bndhqk', qs, kw).astype(jnp.float32)
    sc = jnp.where(mask[None, :, None, None], sc, NEG)
    mx = sc.max(axis=-1, keepdims=True)
    p = jnp.exp(sc - mx)
    den = p.sum(axis=-1, keepdims=True)
    o = jnp.einsum('bndhqk,bnkdhe->bnqdhe', (p / den).astype(v.dtype), vw)
    lse = (mx + jnp.log(den))[..., 0]
    o = o.reshape(b, lp, dilation, h, e)[:, :n].reshape(b, s, h, e)
    lse = lse.transpose(0, 1, 4, 2, 3).reshape(b, lp, dilation, h)[:, :n].reshape(b, s, h)
    return o, lse


def mlstm_chunkwise(q, k, v, i_pre, f_pre):
    b, s, h, dk = q.shape
    dv = v.shape[-1]
    L = MLSTM_CHUNK
    f32 = jnp.float32
    tri = jnp.tril(jnp.ones((L, L), dtype=bool))
    xs = (to_chunks(q.astype(f32), L), to_chunks(k.astype(f32), L), to_chunks(v.astype(f32), L),
          to_chunks(i_pre.astype(f32), L), to_chunks(jax.nn.log_sigmoid(f_pre.astype(f32)), L))

    def step(carry, inp):
        c_st, n_st, m_st = carry
        qc, kc, vc, ic, lf = inp
        bc = jnp.cumsum(lf, axis=-1)
        d_log = jnp.where(tri, bc[..., :, None] - bc[..., None, :] + ic[..., None, :], NEG)
        inter_log = bc + m_st[..., None]
        m_t = jnp.maximum(inter_log, d_log.max(axis=-1))
        a = jnp.einsum('bhte,bhse->bhts', qc, kc) * jnp.exp(d_log - m_t[..., None])
        w_inter = jnp.exp(inter_log - m_t)
        num = jnp.einsum('bhts,bhsv->bhtv', a, vc) + w_inter[..., None] * jnp.einsum('bhte,bhev->bhtv', qc, c_st)
        den = a.sum(axis=-1) + w_inter * jnp.einsum('bhte,bhe->bht', qc, n_st)
        h_t = num / jnp.maximum(jnp.abs(den), jnp.exp(-m_t))[..., None]
        total = bc[..., -1]
        g_log = total[..., None] - bc + ic
        m_new = jnp.maximum(total + m_st, g_log.max(axis=-1))
        w_key = jnp.exp(g_log - m_new[..., None])
        decay = jnp.exp(total + m_st - m_new)
        c_new = decay[..., None, None] * c_st + jnp.einsum('bhs,bhse,bhsv->bhev', w_key, kc, vc)
        n_new = decay[..., None] * n_st + jnp.einsum('bhs,bhse->bhe', w_key, kc)
        return (c_new, n_new, m_new), h_t

    init = (jnp.zeros((b, h, dk, dv), f32), jnp.zeros((b, h, dk), f32), jnp.zeros((b, h), f32))
    _, hs = lax.scan(step, init, xs)
    return from_chunks(hs)


def gla_chunkwise(q, k, v, log_a):
    b, s, h, dk = q.shape
    dv = v.shape[-1]
    L = GLA_CHUNK
    f32 = jnp.float32
    tri = jnp.tril(jnp.ones((L, L), dtype=bool))
    xs = (to_chunks(q.astype(f32), L), to_chunks(k.astype(f32), L), to_chunks(v.astype(f32), L),
          to_chunks(log_a.astype(f32), L))

    def step(state, inp):
        qc, kc, vc, la = inp
        bc = jnp.cumsum(la, axis=-2)
        rel = jnp.where(tri[:, :, None], bc[..., :, None, :] - bc[..., None, :, :], NEG)
        a = jnp.einsum('bhte,bhse,bhtse->bhts', qc, kc, jnp.exp(rel))
        out = jnp.einsum('bhts,bhsv->bhtv', a, vc) + jnp.einsum('bhte,bhev->bhtv', qc * jnp.exp(bc), state)
        total = bc[..., -1, :]
        new = jnp.exp(total)[..., None] * state + jnp.einsum('bhse,bhsv->bhev', kc * jnp.exp(total[..., None, :] - bc), vc)
        return new, out

    _, outs = lax.scan(step, jnp.zeros((b, h, dk, dv), f32), xs)
    return from_chunks(outs)


def hybrid_mixer(h, pos, w_in, conv_w, conv_b, i_bias, f_bias, m_norm, w_decay, b_decay, g_norm, w_out):
    b, s, _ = h.shape
    f32 = jnp.float32
    proj = h @ w_in
    points = np.cumsum(IN_SPLITS)[:-1].tolist()
    (aq, ak, av, mq, mk, mv, mo, mi, mf, gq, gk, gv, gg, gz) = jnp.split(proj, points, axis=-1)

    def heads(t, nh):
        return t.reshape(b, s, nh, -1)

    def rev(t):
        return jnp.flip(t, axis=1)

    aq = rotary(heads(aq, ATT_HEADS), pos) * (ATT_HEAD_DIM ** -0.5)
    ak = rotary(heads(ak, ATT_HEADS), pos)
    av = heads(av, ATT_HEADS)
    outs, lses = [], []
    for window, dilation in DILATED_PATTERNS:
        o, l = dilated_window_attention(aq, ak, av, window, dilation)
        outs.append(o)
        lses.append(l)
    mix_w = jax.nn.softmax(jnp.stack(lses, axis=0), axis=0)
    a_out = jnp.einsum('pbsh,pbshe->bshe', mix_w, jnp.stack(outs, axis=0).astype(f32))
    a_out = a_out.reshape(b, s, ATT_WIDTH).astype(h.dtype)

    qk = jax.nn.silu(centred_depthwise_conv(jnp.concatenate([mq, mk], axis=-1), conv_w) + conv_b)
    mq = heads(qk[..., :MLSTM_QK_WIDTH], MLSTM_HEADS)
    mk = heads(qk[..., MLSTM_QK_WIDTH:], MLSTM_HEADS) * (MLSTM_QK_DIM ** -0.5)
    mv = heads(mv, MLSTM_HEADS)
    mi = (mi.astype(f32) + i_bias.reshape(-1).astype(f32)).reshape(b, s, 2, MLSTM_HEADS)
    mf = (mf.astype(f32) + f_bias.reshape(-1).astype(f32)).reshape(b, s, 2, MLSTM_HEADS)
    m_fwd = mlstm_chunkwise(mq, mk, mv, mi[:, :, 0], mf[:, :, 0])
    m_bwd = rev(mlstm_chunkwise(rev(mq), rev(mk), rev(mv), rev(mi[:, :, 1]), rev(mf[:, :, 1])))
    m_out = jax.nn.sigmoid(mo) * head_norm(m_fwd + m_bwd, m_norm, h.dtype)

    z = jnp.einsum('bsdr,drk->bsdk', gz.reshape(b, s, 2, GLA_RANK), w_decay) + b_decay
    log_a = (jax.nn.log_sigmoid(z.astype(f32)) / GLA_TAU).reshape(b, s, 2, GLA_HEADS, GLA_K_DIM)
    gq = heads(gq, GLA_HEADS) * (GLA_K_DIM ** -0.5)
    gk = heads(gk, GLA_HEADS)
    gv = heads(gv, GLA_HEADS)
    g_fwd = gla_chunkwise(gq, gk, gv, log_a[:, :, 0])
    g_bwd = rev(gla_chunkwise(rev(gq), rev(gk), rev(gv), rev(log_a[:, :, 1])))
    c_out = jax.nn.silu(gg) * head_norm(g_fwd + g_bwd, g_norm, h.dtype)

    return jnp.concatenate([a_out, m_out, c_out], axis=-1) @ w_out


def memory_cross_attention(h, mem_n, wq, wk, wv, wo):
    b, s, _ = h.shape
    m = mem_n.shape[1]
    q = (h @ wq).reshape(b, s, XATT_HEADS, XATT_HEAD_DIM) * (XATT_HEAD_DIM ** -0.5)
    k = (mem_n @ wk).reshape(b, m, XATT_HEADS, XATT_HEAD_DIM)
    v = (mem_n @ wv).reshape(b, m, XATT_HEADS, XATT_HEAD_DIM)
    p = jax.nn.softmax(jnp.einsum('bshe,bmhe->bhsm', q, k).astype(jnp.float32), axis=-1)
    o = jnp.einsum('bhsm,bmhe->bshe', p.astype(v.dtype), v).reshape(b, s, XATT_WIDTH)
    return o @ wo


def squared_relu_mlp(h, w_up, w_down):
    return jnp.square(jax.nn.relu(h @ w_up)) @ w_down


def setup_inputs(seed: int = 0) -> dict:
    key = jax.random.key(seed)
    ks = jax.random.split(key, 26)
    f32 = jnp.float32

    def normal(k, shape):
        return jax.random.normal(k, shape, f32)

    def dense(k, shape, fan_in):
        return normal(k, shape) * (fan_in ** -0.5)

    def gain(k, shape):
        return 1.0 + 0.02 * normal(k, shape)

    f_base = jnp.linspace(3.0, 6.0, MLSTM_HEADS, dtype=f32)
    return {
        'x': normal(ks[0], (BATCH, SEQ, D_MODEL)),
        'mem': normal(ks[1], (BATCH, MEM_LEN, D_MODEL)),
        'norm_mix_pre': gain(ks[2], (DEPTH, D_MODEL)),
        'norm_mix_post': gain(ks[3], (DEPTH, D_MODEL)),
        'w_in': dense(ks[4], (DEPTH, D_MODEL, N_IN), D_MODEL),
        'mlstm_conv_w': dense(ks[5], (DEPTH, CONV_WIDTH, 2 * MLSTM_QK_WIDTH), CONV_WIDTH),
        'mlstm_conv_b': 0.02 * normal(ks[6], (DEPTH, 2 * MLSTM_QK_WIDTH)),
        'mlstm_i_bias': 0.1 * normal(ks[7], (DEPTH, 2, MLSTM_HEADS)),
        'mlstm_f_bias': f_base + 0.1 * normal(ks[8], (DEPTH, 2, MLSTM_HEADS)),
        'mlstm_norm': gain(ks[9], (DEPTH, MLSTM_V_WIDTH)),
        'gla_w_decay': dense(ks[10], (DEPTH, 2, GLA_RANK, GLA_K_WIDTH), GLA_RANK),
        'gla_b_decay': 1.0 + 0.5 * normal(ks[11], (DEPTH, 2, GLA_K_WIDTH)),
        'gla_norm': gain(ks[12], (DEPTH, GLA_V_WIDTH)),
        'w_out': dense(ks[13], (DEPTH, MIX_WIDTH, D_MODEL), MIX_WIDTH),
        'norm_xatt_pre': gain(ks[14], (DEPTH, D_MODEL)),
        'norm_xatt_post': gain(ks[15], (DEPTH, D_MODEL)),
        'norm_mem': gain(ks[16], (DEPTH, D_MODEL)),
        'w_xq': dense(ks[17], (DEPTH, D_MODEL, XATT_WIDTH), D_MODEL),
        'w_xk': dense(ks[18], (DEPTH, D_MODEL, XATT_WIDTH), D_MODEL),
        'w_xv': dense(ks[19], (DEPTH, D_MODEL, XATT_WIDTH), D_MODEL),
        'w_xo': dense(ks[20], (DEPTH, XATT_WIDTH, D_MODEL), XATT_WIDTH),
        'norm_mlp_pre': gain(ks[21], (DEPTH, D_MODEL)),
        'norm_mlp_post': gain(ks[22], (DEPTH, D_MODEL)),
        'w_up': dense(ks[23], (DEPTH, D_MODEL, D_FF), D_MODEL),
        'w_down': dense(ks[24], (DEPTH, D_FF, D_MODEL), D_FF),
    }


def reference(x, mem, norm_mix_pre, norm_mix_post, w_in, mlstm_conv_w, mlstm_conv_b, mlstm_i_bias,
              mlstm_f_bias, mlstm_norm, gla_w_decay, gla_b_decay, gla_norm, w_out, norm_xatt_pre,
              norm_xatt_post, norm_mem, w_xq, w_xk, w_xv, w_xo, norm_mlp_pre, norm_mlp_post, w_up, w_down):
    pos = jnp.arange(x.shape[1])
    for l in range(DEPTH):
        h = rms_norm(x, norm_mix_pre[l])
        y = hybrid_mixer(h, pos, w_in[l], mlstm_conv_w[l], mlstm_conv_b[l], mlstm_i_bias[l], mlstm_f_bias[l],
                         mlstm_norm[l], gla_w_decay[l], gla_b_decay[l], gla_norm[l], w_out[l])
        x = x + rms_norm(y, norm_mix_post[l])
        h = rms_norm(x, norm_xatt_pre[l])
        y = memory_cross_attention(h, rms_norm(mem, norm_mem[l]), w_xq[l], w_xk[l], w_xv[l], w_xo[l])
        x = x + rms_norm(y, norm_xatt_post[l])
        h = rms_norm(x, norm_mlp_pre[l])
        y = squared_relu_mlp(h, w_up[l], w_down[l])
        x = x + rms_norm(y, norm_mlp_post[l])
    return x
```

```python
from contextlib import ExitStack
import ml_dtypes
from concourse.bass_utils import run_bass_kernel_spmd
import numpy as np
import concourse.bass as bass
import concourse.mybir as mybir

F32 = mybir.dt.float32
BF16 = mybir.dt.bfloat16
ALU = mybir.AluOpType
AF = mybir.ActivationFunctionType
AX = mybir.AxisListType

SAME_ENGINE_SYNC = True


class Buf:
    __slots__ = ("name", "last_w", "readers", "sem", "semcnt", "last_dma")

    def __init__(self, name):
        self.name = name
        self.last_w = None
        self.readers = []
        self.sem = None
        self.semcnt = 0
        self.last_dma = None


class Op:
    __slots__ = ("eng", "fn", "deps", "is_dma", "holder", "sig", "dmacnt", "idx")


class Sched:
    ENGS = ("pe", "act", "dve", "pool", "sp")

    def __init__(self, nc):
        self.nc = nc
        self.ops = []

    def add(self, eng, fn, reads=(), writes=(), dma=False, holder=None):
        op = Op()
        op.eng = eng; op.fn = fn; op.is_dma = dma; op.idx = len(self.ops)
        op.sig = 0; op.dmacnt = 0
        deps = set()
        for b in reads:
            if b.last_w is not None:
                deps.add(b.last_w)
        for b in writes:
            if b.last_w is not None:
                deps.add(b.last_w)
            deps.update(b.readers)
        if dma:
            if holder is None:
                holder = writes[0] if writes else reads[0]
            op.holder = holder
            if holder.last_dma is not None:
                deps.add(holder.last_dma)
            holder.last_dma = op.idx
            holder.semcnt += 16
            op.dmacnt = holder.semcnt
        else:
            op.holder = None
        deps.discard(op.idx)
        op.deps = deps
        for b in reads:
            b.readers.append(op.idx)
        for b in writes:
            b.last_w = op.idx
            b.readers = []
        self.ops.append(op)
        return op

    def emit(self, stack):
        nc = self.nc
        ops = self.ops
        need_sig = [False] * len(ops)
        for op in ops:
            best = {}
            for d in op.deps:
                p = ops[d]
                if p.is_dma:
                    continue
                if p.eng == op.eng and (p.eng == "pe" or not SAME_ENGINE_SYNC) and not op.is_dma:
                    continue
                if p.eng not in best or best[p.eng] < d:
                    best[p.eng] = d
            op.deps = set(d for d in op.deps if ops[d].is_dma) | set(best.values())
            for d in best.values():
                need_sig[d] = True
        esem = {}
        for e in self.ENGS:
            esem[e] = stack.enter_context(nc.semaphore("s_" + e))
        ecnt = {e: 0 for e in self.ENGS}
        for i, op in enumerate(ops):
            if op.is_dma:
                if op.holder.sem is None:
                    op.holder.sem = stack.enter_context(nc.semaphore("d_" + op.holder.name))
            elif need_sig[i]:
                ecnt[op.eng] += 1
                op.sig = ecnt[op.eng]
        streams = {e: [] for e in self.ENGS}
        waited = {e: {} for e in self.ENGS}
        for i, op in enumerate(ops):
            waits = {}
            for d in op.deps:
                p = ops[d]
                if p.is_dma:
                    s, v = p.holder.sem, p.dmacnt
                else:
                    if p.eng == op.eng and (p.eng == "pe" or not SAME_ENGINE_SYNC) and not op.is_dma:
                        continue
                    s, v = esem[p.eng], p.sig
                key = id(s)
                if key not in waits or waits[key][1] < v:
                    waits[key] = (s, v)
            wl = []
            w = waited[op.eng]
            for key, (s, v) in waits.items():
                if w.get(key, 0) >= v:
                    continue
                w[key] = v
                wl.append((s, v))
            streams[op.eng].append((wl, op))
        holders = {}
        for op in ops:
            if op.is_dma:
                holders[id(op.holder)] = op.holder
        fin = []
        for hd in holders.values():
            if waited["sp"].get(id(hd.sem), 0) < hd.semcnt:
                fin.append((hd.sem, hd.semcnt))
        fop = Op(); fop.fn = None; fop.is_dma = False; fop.sig = 0
        streams["sp"].append((fin, fop))
        block = stack.enter_context(nc.Block())

        def body_for(e):
            def body(eng):
                for wl, op in streams[e]:
                    for s, v in wl:
                        eng.wait_ge(s, v)
                    if op.fn is None:
                        continue
                    ins = op.fn(eng)
                    if op.is_dma:
                        ins.then_inc(op.holder.sem, 16)
                    elif op.sig:
                        ins.then_inc(esem[e], 1)
            return body
        block.tensor(body_for("pe"))
        block.scalar(body_for("act"))
        block.vector(body_for("dve"))
        block.gpsimd(body_for("pool"))
        block.sync(body_for("sp"))
        self.stats = {e: len(streams[e]) for e in self.ENGS}
        self.semmax = dict(ecnt)


D = 2048
KC = D // 128
NFM = 2560
NTM = 3608
NTMB = 3584
EPS = 1e-6


class Rot:
    def __init__(self, items):
        self.items = items
        self.i = 0

    def next(self):
        it = self.items[self.i % len(self.items)]
        self.i += 1
        return it


def build_A(NT, GT=1024, phases=("fm", "gz", "tm")):
    nc = bass.Bass("TRN2", target_bir_lowering=False)
    dram = lambda n, s, dt, k: nc.dram_tensor(n, s, dt, kind=k).ap()
    x = dram("x", [NT, D], F32, "ExternalInput")
    g_pre = dram("g_pre", [1, D], F32, "ExternalInput")
    w_fm = dram("w_fm", [D, NFM], F32, "ExternalInput")
    w_sw = dram("w_sw", [D, 1024], F32, "ExternalInput")
    w_gz = dram("w_gz", [D, 32], F32, "ExternalInput")
    w_tm = dram("w_tm", [D, NTM], F32, "ExternalInput")
    ifb = dram("ifb", [1, 24], F32, "ExternalInput")
    wdec = dram("wdec", [33, 768], F32, "ExternalInput")
    cosT = dram("cosT", [128, NT], F32, "ExternalInput")
    sinT = dram("sinT", [128, NT], F32, "ExternalInput")
    ident = dram("ident", [128, 128], BF16, "ExternalInput")
    fm = dram("fm", [NFM, NT], BF16, "ExternalOutput")
    tm = dram("tm", [NT, NTMB], BF16, "ExternalOutput")
    gates = dram("gates", [NT, 24], F32, "ExternalOutput")
    gsp = dram("gsp", [NT, 768], F32, "ExternalOutput")

    S = Sched(nc)
    NG = NT // GT
    TPG = GT // 128
    with ExitStack() as st:
        def sb(name, shape, dt, n=1):
            out = []
            for i in range(n):
                t = st.enter_context(nc.sbuf_tensor(f"{name}{i}", shape, dt))
                out.append((t, Buf(f"{name}{i}")))
            return out[0] if n == 1 else Rot(out)

        def ps(name, shape, dt, n=1):
            out = []
            for i in range(n):
                t = st.enter_context(nc.psum_tensor(f"{name}{i}", shape, dt))
                out.append((t, Buf(f"{name}{i}")))
            return out[0] if n == 1 else Rot(out)

        grep_, Bg = sb("grep", [128, D], F32)
        idt, Bid = sb("idt", [128, 128], BF16)
        csts = sb("cst", [128, GT], F32, 2)
        snts = sb("snt", [128, GT], F32, 2)
        ifbt, Bifb = sb("ifbt", [128, 24], F32)
        wdt, Bwd = sb("wdt", [33, 768], F32)
        epst, Beps = sb("epst", [128, 1], F32)
        onet, Bone = sb("onet", [128, 1], F32)
        S.add("sp", lambda e: e.dma_start(out=grep_[:], in_=g_pre.partition_broadcast(128)), writes=[Bg], dma=True)
        S.add("sp", lambda e: e.dma_start(out=idt[:], in_=ident), writes=[Bid], dma=True)
        S.add("sp", lambda e: e.dma_start(out=ifbt[:], in_=ifb.partition_broadcast(128)), writes=[Bifb], dma=True)
        S.add("sp", lambda e: e.dma_start(out=wdt[:], in_=wdec), writes=[Bwd], dma=True)
        S.add("dve", lambda e: e.memset(epst[:], EPS), writes=[Beps])
        S.add("dve", lambda e: e.memset(onet[:], 1.0), writes=[Bone])

        xts = sb("xt", [128, D], F32, 2)
        junk, Bjunk = sb("junk", [128, D], BF16)
        sss = sb("ss", [128, 1], F32, 2)
        rstds = sb("rstd", [128, 1], F32, 2)
        hs = sb("h", [128, D], BF16, 2)
        hT, BhT = sb("hT", [128, KC, GT], BF16)
        wbufs = sb("wb", [128, KC, 512], BF16, 4)
        pTs = ps("pT", [128, 8, 128], BF16, 2)
        pms = ps("pm", [128, 512], F32, 4)
        fstage = sb("fst", [128, 512], BF16, 3)
        sws = sb("sw", [128, 512], F32, 2)
        t1s = sb("t1", [128, 512], F32, 2)
        gzT, BgzT = sb("gzT", [33, GT], F32)
        S.add("dve", lambda e: e.memset(gzT[32:33, :], 1.0), writes=[BgzT])
        gstage = sb("gst", [128, 24], F32, 2)
        g2 = sb("g2", [128, 24], F32, 2)
        zst = sb("zst", [128, 768], F32, 2)
        zst2 = sb("zst2", [128, 768], F32, 2)

        evac_tog = [0]

        def evac_copy(out_ap, in_ap, rb, wb):
            evac_tog[0] ^= 1
            if evac_tog[0]:
                S.add("act", lambda e: e.copy(out=out_ap, in_=in_ap), reads=rb, writes=wb)
            else:
                S.add("dve", lambda e: e.tensor_copy(out=out_ap, in_=in_ap), reads=rb, writes=wb)

        for g in range(NG):
            t0 = g * GT
            cst, Bcs = csts.next(); snt, Bsn = snts.next()
            S.add("sp", lambda e, cst=cst, t0=t0: e.dma_start(out=cst[:], in_=cosT[:, t0:t0 + GT]), writes=[Bcs], dma=True)
            S.add("sp", lambda e, snt=snt, t0=t0: e.dma_start(out=snt[:], in_=sinT[:, t0:t0 + GT]), writes=[Bsn], dma=True)
            for tt in range(TPG):
                r0 = t0 + tt * 128
                xt, Bx = xts.next(); ss, Bss = sss.next(); rstd, Brs = rstds.next(); h, Bh = hs.next()
                S.add("sp", lambda e, xt=xt, r0=r0: e.dma_start(out=xt[:], in_=x[r0:r0 + 128, :]), writes=[Bx], dma=True)
                S.add("act", lambda e, xt=xt, ss=ss: e.activation(out=junk[:], in_=xt[:], func=AF.Square, accum_out=ss[:]),
                      reads=[Bx], writes=[Bjunk, Bss])
                S.add("act", lambda e, ss=ss, rstd=rstd: e.activation(out=rstd[:], in_=ss[:], func=AF.Ln, scale=1.0 / D, bias=epst[:, 0:1]),
                      reads=[Bss, Beps], writes=[Brs])
                S.add("act", lambda e, rstd=rstd: e.activation(out=rstd[:], in_=rstd[:], func=AF.Exp, scale=-0.5),
                      reads=[Brs], writes=[Brs])
                S.add("dve", lambda e, h=h, xt=xt, rstd=rstd: e.scalar_tensor_tensor(out=h[:], in0=xt[:], scalar=rstd[:, 0:1], in1=grep_[:], op0=ALU.mult, op1=ALU.mult),
                      reads=[Bx, Brs, Bg], writes=[Bh])
                for half in range(2):
                    pT, BpT = pTs.next()
                    for c in range(8):
                        kc = half * 8 + c
                        S.add("pe", lambda e, pT=pT, c=c, h=h, kc=kc: e.transpose(out=pT[:, c, :], in_=h[:, kc * 128:(kc + 1) * 128], identity=idt[:]),
                              reads=[Bh, Bid], writes=[BpT])
                    evac_copy(hT[:, half * 8:half * 8 + 8, tt * 128:(tt + 1) * 128], pT[:], [BpT], [BhT])

            def load_w(src, c0, ncols):
                wb, Bw = wbufs.next()
                S.add("pool", lambda e, wb=wb, src=src, c0=c0, ncols=ncols: e.dma_start(
                    out=wb[:, :, 0:ncols], in_=src[:, c0:c0 + ncols].rearrange("(kc p) n -> p kc n", p=128)),
                    writes=[Bw], dma=True)
                return wb, Bw

            for ch in range(NFM // 512 if 'fm' in phases else 0):
                wb, Bw = load_w(w_fm, ch * 512, 512)
                if ch < 2:
                    wb2, Bw2 = load_w(w_sw, ch * 512, 512)
                for jj in range(4):
                    j = ch * 4 + jj
                    for s in range(GT // 512):
                        tk = s * 512
                        pm, Bpm = pms.next()
                        for kc in range(KC):
                            S.add("pe", lambda e, pm=pm, wb=wb, jj=jj, kc=kc, tk=tk: e.matmul(
                                pm[:], lhsT=wb[:, kc, jj * 128:(jj + 1) * 128], rhs=hT[:, kc, tk:tk + 512],
                                start=(kc == 0), stop=(kc == KC - 1)), reads=[Bw, BhT], writes=[Bpm])
                        fs, Bfs = fstage.next()
                        if j < 8:
                            pm2, Bpm2 = pms.next()
                            for kc in range(KC):
                                S.add("pe", lambda e, pm2=pm2, wb2=wb2, jj=jj, kc=kc, tk=tk: e.matmul(
                                    pm2[:], lhsT=wb2[:, kc, jj * 128:(jj + 1) * 128], rhs=hT[:, kc, tk:tk + 512],
                                    start=(kc == 0), stop=(kc == KC - 1)), reads=[Bw2, BhT], writes=[Bpm2])
                            sw, Bsw = sws.next(); t1, Bt1 = t1s.next()
                            c0 = tk
                            S.add("dve", lambda e, t1=t1, pm=pm, c0=c0, cst=cst: e.tensor_tensor(out=t1[:], in0=pm[:], in1=cst[:, c0:c0 + 512], op=ALU.mult),
                                  reads=[Bpm, Bcs], writes=[Bt1])
                            S.add("dve", lambda e, sw=sw, pm2=pm2, c0=c0, snt=snt: e.tensor_tensor(out=sw[:], in0=pm2[:], in1=snt[:, c0:c0 + 512], op=ALU.mult),
                                  reads=[Bpm2, Bsn], writes=[Bsw])
                            S.add("dve", lambda e, fs=fs, t1=t1, sw=sw: e.tensor_tensor(out=fs[:], in0=t1[:], in1=sw[:], op=ALU.add),
                                  reads=[Bt1, Bsw], writes=[Bfs])
                        else:
                            evac_copy(fs[:], pm[:], [Bpm], [Bfs])
                        S.add("sp", lambda e, fs=fs, j=j, c0=t0 + tk: e.dma_start(out=fm[j * 128:(j + 1) * 128, c0:c0 + 512], in_=fs[:]),
                              reads=[Bfs], dma=True)
            if 'gz' in phases:
                wb, Bw = load_w(w_gz, 0, 32)
            for s in range(GT // 512 if 'gz' in phases else 0):
                tk = s * 512
                pm, Bpm = pms.next()
                for kc in range(KC):
                    S.add("pe", lambda e, pm=pm, wb=wb, kc=kc, tk=tk: e.matmul(
                        pm[0:32, :], lhsT=wb[:, kc, 0:32], rhs=hT[:, kc, tk:tk + 512],
                        start=(kc == 0), stop=(kc == KC - 1)), reads=[Bw, BhT], writes=[Bpm])
                evac_copy(gzT[0:32, tk:tk + 512], pm[0:32, :], [Bpm], [BgzT])
            for tt in range(TPG if 'gz' in phases else 0):
                zs, Bzs = zst.next(); zs2, Bzs2 = zst2.next()
                for hh in range(2):
                    pm, Bpm = pms.next()
                    S.add("pe", lambda e, pm=pm, tt=tt, hh=hh: e.matmul(
                        pm[:, 0:384], lhsT=gzT[0:33, tt * 128:(tt + 1) * 128], rhs=wdt[0:33, hh * 384:(hh + 1) * 384],
                        start=True, stop=True), reads=[BgzT, Bwd], writes=[Bpm])
                    S.add("act", lambda e, zs=zs, pm=pm, hh=hh: e.activation(out=zs[:, hh * 384:(hh + 1) * 384], in_=pm[:, 0:384], func=AF.Exp, scale=-1.0),
                          reads=[Bpm], writes=[Bzs])
                S.add("act", lambda e, zs=zs, zs2=zs2: e.activation(out=zs2[:], in_=zs[:], func=AF.Ln, bias=onet[:, 0:1]),
                      reads=[Bzs, Bone], writes=[Bzs2])
                r0 = t0 + tt * 128
                S.add("sp", lambda e, zs2=zs2, r0=r0: e.dma_start(out=gsp[r0:r0 + 128, :], in_=zs2[:]), reads=[Bzs2], dma=True)
            for ch in range(8 if 'tm' in phases else 0):
                ncols = 512 if ch < 7 else 24
                wb, Bw = load_w(w_tm, ch * 512, ncols)
                for tt in range(TPG):
                    r0 = t0 + tt * 128
                    pm, Bpm = pms.next()
                    for kc in range(KC):
                        S.add("pe", lambda e, pm=pm, wb=wb, kc=kc, tt=tt, ncols=ncols: e.matmul(
                            pm[:, 0:ncols], lhsT=hT[:, kc, tt * 128:(tt + 1) * 128], rhs=wb[:, kc, 0:ncols],
                            start=(kc == 0), stop=(kc == KC - 1)), reads=[Bw, BhT], writes=[Bpm])
                    if ch < 7:
                        fs, Bfs = fstage.next()
                        lo = ch * 512
                        segs = []
                        for (a, b, fn) in ((0, 2048, None), (2048, 2816, AF.Sigmoid), (2816, 3584, AF.Silu)):
                            a2, b2 = max(a, lo), min(b, lo + 512)
                            if a2 < b2:
                                segs.append((a2 - lo, b2 - lo, fn))
                        for (a, b, fn) in segs:
                            if fn is None:
                                evac_copy(fs[:, a:b], pm[:, a:b], [Bpm], [Bfs])
                            else:
                                S.add("act", lambda e, fs=fs, pm=pm, a=a, b=b, fn=fn: e.activation(out=fs[:, a:b], in_=pm[:, a:b], func=fn),
                                      reads=[Bpm], writes=[Bfs])
                        S.add("sp", lambda e, fs=fs, r0=r0, lo=lo: e.dma_start(out=tm[r0:r0 + 128, lo:lo + 512], in_=fs[:]),
                              reads=[Bfs], dma=True)
                    else:
                        gs, Bgs = gstage.next(); gq, Bgq = g2.next()
                        S.add("dve", lambda e, gs=gs, pm=pm: e.tensor_tensor(out=gs[:], in0=pm[:, 0:24], in1=ifbt[:], op=ALU.add),
                              reads=[Bpm, Bifb], writes=[Bgs])
                        S.add("act", lambda e, gs=gs, gq=gq: e.activation(out=gq[:, 12:24], in_=gs[:, 0:12], func=AF.Exp),
                              reads=[Bgs], writes=[Bgq])
                        S.add("act", lambda e, gs=gs: e.activation(out=gs[:, 12:24], in_=gs[:, 12:24], func=AF.Exp, scale=-1.0),
                              reads=[Bgs], writes=[Bgs])
                        S.add("act", lambda e, gs=gs, gq=gq: e.activation(out=gq[:, 0:12], in_=gs[:, 12:24], func=AF.Ln, bias=onet[:, 0:1]),
                              reads=[Bgs, Bone], writes=[Bgq])
                        S.add("sp", lambda e, gq=gq, r0=r0: e.dma_start(out=gates[r0:r0 + 128, :], in_=gq[:]), reads=[Bgq], dma=True)
        S.emit(st)
    return nc, S


DILS = (1, 4, 16)


def build_B1(S_):
    nc = bass.Bass("TRN2", target_bir_lowering=False)
    dram = lambda n, s, dt, k: nc.dram_tensor(n, s, dt, kind=k).ap()
    NTL = S_ // 128
    qTd = dram("qT", [3, 128, S_], BF16, "ExternalInput")
    kTd = dram("kT", [3, 128, S_], BF16, "ExternalInput")
    vd = dram("v", [3, S_, 128], BF16, "ExternalInput")
    maskd = dram("mask", [128, 384], BF16, "ExternalInput")
    od = dram("o", [3, S_, 129], F32, "ExternalOutput")
    SCALE = 128.0 ** -0.5
    S = Sched(nc)
    with ExitStack() as st:
        def sb(name, shape, dt, n=1):
            out = []
            for i in range(n):
                t = st.enter_context(nc.sbuf_tensor(f"{name}{i}", shape, dt))
                out.append((t, Buf(f"{name}{i}")))
            return out[0] if n == 1 else Rot(out)

        def ps(name, shape, dt, n=1):
            out = []
            for i in range(n):
                t = st.enter_context(nc.psum_tensor(f"{name}{i}", shape, dt))
                out.append((t, Buf(f"{name}{i}")))
            return out[0] if n == 1 else Rot(out)

        mask, Bmask = sb("mask", [128, 384], BF16)
        S.add("sp", lambda e: e.dma_start(out=mask[:], in_=maskd), writes=[Bmask], dma=True)
        qT, BqT = sb("qT", [128, S_], BF16)
        kT, BkT = sb("kT", [128, S_], BF16)
        NVQ = 4
        vq = [sb(f"vq{i}", [128, NTL // NVQ, 129], BF16) for i in range(NVQ)]
        for i in range(NVQ):
            S.add("dve", lambda e, i=i: e.memset(vq[i][0][:, :, 128:129], 1.0), writes=[vq[i][1]])
        Es = sb("E", [128, 384], BF16, 3)
        Ems = sb("Em", [128, 384], BF16, 3)
        ost = sb("ost", [128, 8, 129], F32, 2)
        pS = ps("pS", [128, 512], F32, 2)
        pO = [ps(f"pO{i}", [128, 512], F32) for i in range(4)]
        tog = [0]
        for p, d in enumerate(DILS):
            n = S_ // d
            T = n // 128
            S.add("sp", lambda e, p=p: e.dma_start(out=qT[:], in_=qTd[p]), writes=[BqT], dma=True)
            S.add("sp", lambda e, p=p: e.dma_start(out=kT[:], in_=kTd[p]), writes=[BkT], dma=True)
            per = NTL // NVQ
            for i in range(NVQ):
                S.add("sp", lambda e, p=p, i=i: e.dma_start(out=vq[i][0][:, :, 0:128],
                      in_=vd[p, i * per * 128:(i + 1) * per * 128, :].rearrange("(j a) e -> a j e", a=128)), writes=[vq[i][1]], dma=True)
            os_, Bos = None, None
            for seg in range(d):
                for j in range(T):
                    g = seg * T + j
                    jl, jh = max(j - 1, 0), min(j + 1, T - 1)
                    q0 = (seg * T + jl) * 128
                    nq = (jh - jl + 1) * 128
                    mc0 = (jl - (j - 1)) * 128
                    ps_, Bps = pS.next()
                    S.add("pe", lambda e, ps_=ps_, g=g, q0=q0, nq=nq: e.matmul(ps_[:, 0:nq], lhsT=kT[:, g * 128:(g + 1) * 128], rhs=qT[:, q0:q0 + nq], start=True, stop=True),
                          reads=[BkT, BqT], writes=[Bps])
                    E, BE = Es.next(); Em, BEm = Ems.next()
                    S.add("act", lambda e, E=E, ps_=ps_, nq=nq: e.activation(out=E[:, 0:nq], in_=ps_[:, 0:nq], func=AF.Exp, scale=SCALE), reads=[Bps], writes=[BE])
                    S.add("dve", lambda e, E=E, Em=Em, nq=nq, mc0=mc0: e.tensor_tensor(out=Em[:, 0:nq], in0=E[:, 0:nq], in1=mask[:, mc0:mc0 + nq], op=ALU.mult),
                          reads=[BE, Bmask], writes=[BEm])
                    vt, Bvt = vq[g // per]
                    gi = g % per
                    for t in range(jl, jh + 1):
                        gt = seg * T + t
                        po, Bpo = pO[gt % 4]
                        c0 = (t - jl) * 128
                        first = (j == max(t - 1, 0)); last = (j == min(t + 1, T - 1))
                        S.add("pe", lambda e, po=po, Em=Em, c0=c0, vt=vt, gi=gi, first=first, last=last: e.matmul(
                            po[:, 0:129], lhsT=Em[:, c0:c0 + 128], rhs=vt[:, gi, :], start=first, stop=last), reads=[BEm, Bvt], writes=[Bpo])
                    done = [j - 1] if j >= 1 else []
                    if j == T - 1:
                        done.append(j)
                    for t in done:
                        gt = seg * T + t
                        po, Bpo = pO[gt % 4]
                        if gt % 8 == 0:
                            os_, Bos = ost.next()
                        tog[0] ^= 1
                        if tog[0]:
                            S.add("act", lambda e, os_=os_, po=po, gt=gt: e.copy(out=os_[:, gt % 8, :], in_=po[:, 0:129]), reads=[Bpo], writes=[Bos])
                        else:
                            S.add("dve", lambda e, os_=os_, po=po, gt=gt: e.tensor_copy(out=os_[:, gt % 8, :], in_=po[:, 0:129]), reads=[Bpo], writes=[Bos])
                        if gt % 8 == 7:
                            r0 = (gt - 7) * 128
                            S.add("sp", lambda e, os_=os_, p=p, r0=r0: e.dma_start(out=od[p, r0:r0 + 1024, :].rearrange("(j a) e -> a j e", a=128), in_=os_[:]),
                                  reads=[Bos], dma=True)
        S.emit(st)
    return nc, S


L = 128
BLK = 4


def build_B2(S_, NU=3):
    nc = bass.Bass("TRN2", target_bir_lowering=False)
    dram = lambda n, s, dt, k: nc.dram_tensor(n, s, dt, kind=k).ap()
    NCH = S_ // L
    NB = NCH // BLK
    BT = BLK * L
    mqk = dram("mqk", [NU, 2, 64, S_ + 4], BF16, "ExternalInput")
    cw = dram("cw", [NU, 64, 10], F32, "ExternalInput")
    cb = dram("cb", [NU, 64, 2], F32, "ExternalInput")
    nlf = dram("nlf", [NU, 128, NCH], F32, "ExternalInput")
    ei = dram("ei", [NU, 128, NCH], F32, "ExternalInput")
    mv = dram("mv", [NU, S_, 128], BF16, "ExternalInput")
    gqk = dram("gqk", [NU, 2, 64, S_], BF16, "ExternalInput")
    gsp = dram("gsp", [NU, S_, 64], F32, "ExternalInput")
    gv = dram("gv", [NU, S_, 128], BF16, "ExternalInput")
    triU = dram("triU", [128, 128], F32, "ExternalInput")
    ident = dram("ident", [128, 128], BF16, "ExternalInput")
    ym = dram("ym", [NU, S_, 128], F32, "ExternalOutput")
    yg = dram("yg", [NU, S_, 128], F32, "ExternalOutput")

    S = Sched(nc)
    with ExitStack() as st:
        def sb(name, shape, dt, n=1):
            out = []
            for i in range(n):
                t = st.enter_context(nc.sbuf_tensor(f"{name}{i}", shape, dt))
                out.append((t, Buf(f"{name}{i}")))
            return out[0] if n == 1 else Rot(out)

        def ps(name, shape, dt, n=1):
            out = []
            for i in range(n):
                t = st.enter_context(nc.psum_tensor(f"{name}{i}", shape, dt))
                out.append((t, Buf(f"{name}{i}")))
            return out[0] if n == 1 else Rot(out)

        tri, Btri = sb("tri", [128, 128], F32)
        idt, Bid = sb("idt", [128, 128], BF16)
        S.add("sp", lambda e: e.dma_start(out=tri[:], in_=triU), writes=[Btri], dma=True)
        S.add("sp", lambda e: e.dma_start(out=idt[:], in_=ident), writes=[Bid], dma=True)

        class G:
            pass
        groups = []
        for kind in ("m", "g"):
            o = G()
            o.kind = kind
            o.NV = 129 if kind == "m" else 128
            o.sc = 1.0 if kind == "m" else 1.0 / 16.0
            o.yout = ym if kind == "m" else yg
            if kind == "m":
                o.cwt, o.Bcw = sb("cwt", [64, NU, 10], F32)
                o.cbt, o.Bcb = sb("cbt", [64, NU, 2], F32)
                o.dg, o.Bdg = sb("dg", [64, NU * 10, 64], BF16)
                o.nlft, o.Bnlf = sb("nlft", [128, NU, NCH], F32)
                o.eit, o.Bei = sb("eit", [128, NU, NCH], F32)
                o.xps = [sb(f"xp{i}_", [64, NU, BT + 4], BF16, 2) for i in range(2)]
                o.vraw = sb("vraw", [128, NU, BLK, 128], BF16, 2)
            else:
                o.spb = sb("spb", [128, NU, BLK, 64], F32, 2)
            o.qks = [sb(f"qk{kind}{i}_", [64, NU, BT], BF16, 2) for i in range(2)]
            o.vpp = sb("vpp" + kind, [128, NU, BLK, 129], BF16, 2)
            o.ost = sb("ost" + kind, [128, NU, BLK, 128], F32, 2)
            o.Ct, o.BC = sb("C" + kind, [64, NU, 129], F32)
            o.Cbfs = sb("Cbf" + kind, [64, NU, 129], BF16, 2)
            o.Eqs = sb("Eq" + kind, [64, NU, 128], F32, 4)
            o.Eks = sb("Ek" + kind, [64, NU, 128], F32, 4)
            o.qss = sb("qs" + kind, [64, NU, 128], BF16, 4)
            o.kss = sb("ks" + kind, [64, NU, 128], BF16, 4)
            o.Ams = sb("Am" + kind, [128, NU, 128], BF16, 3)
            o.ksts = sb("kst" + kind, [128, NU, 64], BF16, 3)
            o.rs = sb("r" + kind, [128, NU], F32, 3)
            groups.append(o)

        pcs = ps("pc", [64, 512], F32, 1)
        pas = ps("pa", [128, 512], F32, 2)
        pos = ps("po", [128, 512], F32, 2)
        pps = ps("pp", [64, 512], F32, 1)
        pts = ps("pt", [128, 1024], BF16, 1)
        pcv = ps("pcv", [64, 512], F32, 1)
        PO = 160

        for o in groups:
            if o.kind == "m":
                S.add("sp", lambda e, o=o: e.dma_start(out=o.cwt[:], in_=cw.rearrange("u p k -> p u k")), writes=[o.Bcw], dma=True)
                S.add("sp", lambda e, o=o: e.dma_start(out=o.cbt[:], in_=cb.rearrange("u p k -> p u k")), writes=[o.Bcb], dma=True)
                S.add("sp", lambda e, o=o: e.dma_start(out=o.nlft[:], in_=nlf.rearrange("u p k -> p u k")), writes=[o.Bnlf], dma=True)
                S.add("sp", lambda e, o=o: e.dma_start(out=o.eit[:], in_=ei.rearrange("u p k -> p u k")), writes=[o.Bei], dma=True)
                for u in range(NU):
                    for j in range(10):
                        S.add("dve", lambda e, o=o, u=u, j=j: e.tensor_scalar(out=o.dg[:, u * 10 + j, :], in0=idt[0:64, 0:64], scalar1=o.cwt[:, u, j:j + 1], scalar2=None, op0=ALU.mult),
                              reads=[Bid, o.Bcw], writes=[o.Bdg])
            S.add("dve", lambda e, o=o: e.memset(o.Ct[:], 0.0), writes=[o.BC])
            o.Cbf, o.BCbf = None, None

        bs = {}
        chs = {}

        def load_block(blk):
            b0 = blk * BT
            for o in groups:
                kind = o.kind
                B_ = G()
                qk_t = []
                for i in range(2):
                    qt, Bq = o.qks[i].next()
                    if kind == "m":
                        xp, Bxp = o.xps[i].next()
                        S.add("sp", lambda e, xp=xp, i=i: e.dma_start(out=xp[:], in_=mqk[:, i, :, b0:b0 + BT + 4].rearrange("u p t -> p u t")), writes=[Bxp], dma=True)
                        for u in range(NU):
                            pv, Bpv = pcv
                            for j in range(5):
                                S.add("pe", lambda e, pv=pv, o=o, u=u, i=i, j=j, xp=xp: e.matmul(pv[:, 0:BT], lhsT=o.dg[:, u * 10 + 5 * i + j, :], rhs=xp[:, u, j:j + BT],
                                      start=(j == 0), stop=(j == 4)), reads=[o.Bdg, Bxp], writes=[Bpv])
                            S.add("act", lambda e, qt=qt, pv=pv, o=o, u=u, i=i: e.activation(out=qt[:, u, :], in_=pv[:, 0:BT], func=AF.Silu, bias=o.cbt[:, u, i:i + 1]),
                                  reads=[Bpv, o.Bcb], writes=[Bq])
                    else:
                        S.add("sp", lambda e, qt=qt, i=i: e.dma_start(out=qt[:], in_=gqk[:, i, :, b0:b0 + BT].rearrange("u p t -> p u t")), writes=[Bq], dma=True)
                    qk_t.append((qt, Bq))
                (B_.qT, B_.BqT), (B_.kT, B_.BkT) = qk_t
                vp, Bvp = o.vpp.next()
                B_.vp, B_.Bvp = vp, Bvp
                if kind == "m":
                    vr, Bvr = o.vraw.next()
                    for u in range(NU):
                        S.add("sp", lambda e, vr=vr, u=u: e.dma_start(out=vr[:, u], in_=mv[u, b0:b0 + BT, :].rearrange("(c a) e -> a c e", a=128)), writes=[Bvr], dma=True)
                    esl = o.eit[:, :, blk * BLK:(blk + 1) * BLK]
                    S.add("dve", lambda e, vp=vp, vr=vr, esl=esl: e.tensor_tensor(out=vp[:, :, :, 0:128], in0=vr[:], in1=esl.unsqueeze(3).to_broadcast([128, NU, BLK, 128]), op=ALU.mult),
                          reads=[Bvr, o.Bei], writes=[Bvp])
                    S.add("dve", lambda e, vp=vp, esl=esl: e.tensor_copy(out=vp[:, :, :, 128], in_=esl), reads=[o.Bei], writes=[Bvp])
                else:
                    for u in range(NU):
                        S.add("sp", lambda e, vp=vp, u=u: e.dma_start(out=vp[:, u, :, 0:128], in_=gv[u, b0:b0 + BT, :].rearrange("(c a) e -> a c e", a=128)), writes=[Bvp], dma=True)
                    B_.sp_, B_.Bsp = o.spb.next()
                    for u in range(NU):
                        S.add("sp", lambda e, sp_=B_.sp_, u=u: e.dma_start(out=sp_[:, u], in_=gsp[u, b0:b0 + BT, :].rearrange("(c a) e -> a c e", a=128)), writes=[B_.Bsp], dma=True)
                B_.os_, B_.Bos = o.ost.next()
                bs[(kind, blk)] = B_

        def s1(o, cg):
            kind, NV, sc = o.kind, o.NV, o.sc
            blk, c = cg // BLK, cg % BLK
            csl = slice(c * L, (c + 1) * L)
            B_ = bs[(kind, blk)]
            H = G(); chs[(kind, cg)] = H
            pc, Bpc = pcs
            for u in range(NU):
                if kind == "m":
                    S.add("pe", lambda e, pc=pc, o=o, u=u: e.matmul(pc[:, u * 128:(u + 1) * 128], lhsT=o.nlft[:, u, cg:cg + 1].to_broadcast([128, 64]), rhs=tri[:], start=True, stop=True),
                          reads=[o.Bnlf, Btri], writes=[Bpc])
                else:
                    S.add("pe", lambda e, pc=pc, sp_=B_.sp_, u=u: e.matmul(pc[:, u * 128:(u + 1) * 128], lhsT=sp_[:, u, c, :], rhs=tri[:], start=True, stop=True),
                          reads=[B_.Bsp, Btri], writes=[Bpc])
            H.Eq, H.BEq = o.Eqs.next(); Ek, BEk = o.Eks.next()
            pcv3 = pc[:, 0:NU * 128].rearrange("p (u t) -> p u t", u=NU)
            S.add("act", lambda e, Eq=H.Eq, pcv3=pcv3: e.activation(out=Eq[:], in_=pcv3, func=AF.Exp, scale=-sc), reads=[Bpc], writes=[H.BEq])
            S.add("act", lambda e, Ek=Ek, pcv3=pcv3: e.activation(out=Ek[:], in_=pcv3, func=AF.Exp, scale=sc), reads=[Bpc], writes=[BEk])
            H.qs, H.Bqs = o.qss.next(); H.ks, H.Bks = o.kss.next()
            S.add("pool", lambda e, qs=H.qs, qT=B_.qT, Eq=H.Eq: e.tensor_tensor(out=qs[:], in0=qT[:, :, csl], in1=Eq[:], op=ALU.mult), reads=[B_.BqT, H.BEq], writes=[H.Bqs])
            S.add("pool", lambda e, ks=H.ks, kT=B_.kT, Ek=Ek: e.tensor_tensor(out=ks[:], in0=kT[:, :, csl], in1=Ek[:], op=ALU.mult), reads=[B_.BkT, BEk], writes=[H.Bks])

        def s2(o, cg):
            kind = o.kind
            H = chs[(kind, cg)]
            qs, Bqs, ks, Bks = H.qs, H.Bqs, H.ks, H.Bks
            pa, Bpa = pas.next()
            for u in range(NU):
                S.add("pe", lambda e, pa=pa, u=u: e.matmul(pa[:, u * 128:(u + 1) * 128], lhsT=ks[:, u, :], rhs=qs[:, u, :], start=True, stop=True),
                      reads=[Bks, Bqs], writes=[Bpa])
            H.Am, H.BAm = o.Ams.next()
            S.add("dve", lambda e, Am=H.Am, pa=pa: e.tensor_tensor(out=Am[:], in0=pa[:, 0:NU * 128].rearrange("p (u t) -> p u t", u=NU),
                                                               in1=tri[:].unsqueeze(1).to_broadcast([128, NU, 128]), op=ALU.mult), reads=[Bpa, Btri], writes=[H.BAm])
            pt, Bpt = pts
            for u in range(NU):
                S.add("pe", lambda e, pt=pt, u=u: e.transpose(out=pt[:, u * 64:(u + 1) * 64], in_=ks[:, u, :], identity=idt[0:64, 0:64]), reads=[Bks, Bid], writes=[Bpt])
            H.kst, H.Bkst = o.ksts.next()
            S.add("act", lambda e, kst=H.kst, pt=pt: e.copy(out=kst[:], in_=pt[:, 0:NU * 64].rearrange("p (u t) -> p u t", u=NU)), reads=[Bpt], writes=[H.Bkst])

        def s3(o, cg):
            kind, NV = o.kind, o.NV
            blk, c = cg // BLK, cg % BLK
            first = (cg == 0)
            B_ = bs[(kind, blk)]
            H = chs.pop((kind, cg))
            Am, BAm, qs, Bqs, kst, Bkst, Eq, BEq = H.Am, H.BAm, H.qs, H.Bqs, H.kst, H.Bkst, H.Eq, H.BEq
            vp, Bvp = B_.vp, B_.Bvp
            Cbf_prev, BCbf_prev = o.Cbf, o.BCbf
            po, Bpo = pos.next()
            for u in range(NU):
                S.add("pe", lambda e, po=po, u=u: e.matmul(po[:, u * PO:u * PO + NV], lhsT=Am[:, u, :], rhs=vp[:, u, c, 0:NV], start=True, stop=first),
                      reads=[BAm, Bvp], writes=[Bpo])
                if not first:
                    S.add("pe", lambda e, po=po, u=u: e.matmul(po[:, u * PO:u * PO + NV], lhsT=qs[:, u, :], rhs=Cbf_prev[:, u, 0:NV], start=False, stop=True),
                          reads=[Bqs, BCbf_prev], writes=[Bpo])
            pp, Bpp = pps
            for u in range(NU):
                S.add("pe", lambda e, pp=pp, u=u: e.matmul(pp[:, u * PO:u * PO + NV], lhsT=kst[:, u, :], rhs=vp[:, u, c, 0:NV], start=True, stop=True),
                      reads=[Bkst, Bvp], writes=[Bpp])
            ppv = pp[:, 0:NU * PO].rearrange("p (u t) -> p u t", u=NU)[:, :, 0:NV]
            pov = po[:, 0:NU * PO].rearrange("p (u t) -> p u t", u=NU)
            S.add("dve", lambda e: e.tensor_tensor(out=o.Ct[:, :, 0:NV], in0=ppv, in1=o.Ct[:, :, 0:NV], op=ALU.add), reads=[Bpp, o.BC], writes=[o.BC])
            S.add("dve", lambda e: e.tensor_tensor(out=o.Ct[:, :, 0:NV], in0=o.Ct[:, :, 0:NV], in1=Eq[:, :, 127:128].to_broadcast([64, NU, NV]), op=ALU.mult),
                  reads=[o.BC, BEq], writes=[o.BC])
            o.Cbf, o.BCbf = o.Cbfs.next()
            S.add("pool", lambda e, Cbf=o.Cbf: e.tensor_copy(out=Cbf[:, :, 0:NV], in_=o.Ct[:, :, 0:NV]), reads=[o.BC], writes=[o.BCbf])
            os_, Bos = B_.os_, B_.Bos
            if kind == "m":
                r, Br = o.rs.next()
                S.add("act", lambda e: e.activation(out=r[:], in_=pov[:, :, 128], func=AF.Abs), reads=[Bpo], writes=[Br])
                S.add("dve", lambda e: e.tensor_scalar(out=r[:], in0=r[:], scalar1=8.0, scalar2=None, op0=ALU.max), reads=[Br], writes=[Br])
                S.add("dve", lambda e: e.reciprocal(out=r[:], in_=r[:]), reads=[Br], writes=[Br])
                S.add("dve", lambda e: e.tensor_tensor(out=os_[:, :, c, :], in0=pov[:, :, 0:128], in1=r[:].unsqueeze(2).to_broadcast([128, NU, 128]), op=ALU.mult),
                      reads=[Bpo, Br], writes=[Bos])
            else:
                S.add("act", lambda e: e.activation(out=os_[:, :, c, :], in_=pov[:, :, 0:128], func=AF.Copy, scale=0.125), reads=[Bpo], writes=[Bos])
            if c == BLK - 1:
                b0 = blk * BT
                for u in range(NU):
                    S.add("sp", lambda e, u=u: e.dma_start(out=o.yout[u, b0:b0 + BT, :].rearrange("(c a) e -> a c e", a=128), in_=os_[:, u]),
                          reads=[Bos], dma=True)

        for t in range(NCH + 2):
            if t < NCH and t % BLK == 0:
                load_block(t // BLK)
            for o in groups:
                if t < NCH:
                    s1(o, t)
                if 0 <= t - 1 < NCH:
                    s2(o, t - 1)
                if 0 <= t - 2 < NCH:
                    s3(o, t - 2)
        S.emit(st)
    return nc, S


D = 2048
KC = 16
DFF = 8192
EPS = 1e-6
MEM = 256


def build_C(NT, GT=256):
    nc = bass.Bass("TRN2", target_bir_lowering=False)
    dram = lambda n, s, dt, k: nc.dram_tensor(n, s, dt, kind=k).ap()
    I = "ExternalInput"
    x = dram("x", [NT, D], F32, I)
    att = dram("att", [NT, 12 * 129], F32, I)
    mfb = dram("mfb", [2, NT, 768], F32, I)
    gfb = dram("gfb", [2, NT, 768], F32, I)
    gate = dram("gate", [NT, 1536], BF16, I)
    hn = dram("hn", [2, 768], F32, I)
    gains = dram("gains", [6, D], F32, I)
    mem = dram("mem", [MEM, D], F32, I)
    w_out = dram("w_out", [D, D], F32, I)
    w_xq = dram("w_xq", [D, 512], F32, I)
    w_xk = dram("w_xk", [D, 512], F32, I)
    w_xv = dram("w_xv", [D, 512], F32, I)
    w_xo = dram("w_xo", [512, D], F32, I)
    w_up = dram("w_up", [D, DFF], F32, I)
    w_down = dram("w_down", [DFF, D], F32, I)
    ident = dram("ident", [128, 128], BF16, I)
    xo = dram("xo", [NT, D], F32, "ExternalOutput")

    WSPEC = [("xk", w_xk, D, 512), ("xv", w_xv, D, 512), ("out", w_out, D, D), ("xq", w_xq, D, 512),
             ("xo", w_xo, 512, D), ("up", w_up, D, DFF), ("down", w_down, DFF, D)]
    WB = {}
    for nm, src, nr, ncol in WSPEC:
        WB[nm] = (nc.dram_tensor("wbf_" + nm, [nr, ncol], BF16).ap(), [Buf(f"wbf_{nm}_{i}") for i in range(nr // 128)])

    S = Sched(nc)
    NG = NT // GT
    TPG = GT // 128
    XS = 128.0 ** -0.5
    with ExitStack() as st:
        def sb(name, shape, dt, n=1):
            out = []
            for i in range(n):
                t = st.enter_context(nc.sbuf_tensor(f"{name}{i}", shape, dt))
                out.append((t, Buf(f"{name}{i}")))
            return out[0] if n == 1 else Rot(out)

        def ps(name, shape, dt, n=1):
            out = []
            for i in range(n):
                t = st.enter_context(nc.psum_tensor(f"{name}{i}", shape, dt))
                out.append((t, Buf(f"{name}{i}")))
            return out[0] if n == 1 else Rot(out)

        idt, Bid = sb("idt", [128, 128], BF16)
        S.add("sp", lambda e: e.dma_start(out=idt[:], in_=ident), writes=[Bid], dma=True)
        epst, Beps = sb("epst", [128, 1], F32)
        S.add("dve", lambda e: e.memset(epst[:], EPS), writes=[Beps])
        hnt, Bhn = sb("hnt", [128, 2, 768], F32)
        for i in range(2):
            S.add("sp", lambda e, i=i: e.dma_start(out=hnt[:, i, :], in_=hn[i:i + 1, :].partition_broadcast(128)), writes=[Bhn], dma=True, holder=Buf(f"hnl{i}"))
        greps = sb("grep", [128, D], F32, 2)
        xts = [sb(f"xt{i}", [128, D], F32) for i in range(TPG)]
        yts = [sb(f"yt{i}", [128, D], F32) for i in range(TPG)]
        actT, BactT = sb("actT", [128, KC, GT], BF16)
        aT, BaT = sb("aT", [128, DFF // 128, GT], BF16)
        wbufs = sb("wb", [128, KC, 512], BF16, 3)
        junk, Bjunk = sb("junk", [128, D], BF16)
        sss = sb("ss", [128, 8], F32, 3)
        hs = sb("h", [128, D], BF16, 2)
        attt = Rot([sb("attt", [128, 12 * 129], F32)])
        ssum = Rot([sb("ssum", [128, 4 * 129], F32)])
        fbt = Rot([sb("fbt", [128, 2, 768], F32)])
        tsum = Rot([sb("tsum", [128, 768], F32)])
        tsq = Rot([sb("tsq", [128, 768], F32)])
        gtt = Rot([sb("gtt", [128, 1536], BF16)])
        rr = sb("rr", [128, 8], F32, 4)
        KTs, BKT = sb("KTs", [128, 4, MEM], BF16)
        Vs, BV = sb("Vs", [128, 2, 4, 129], BF16)
        qTs, BqT = sb("qTs", [128, 4, GT], BF16)
        Es = sb("E", [128, GT], BF16, 4)
        ots = sb("ot", [128, 512], BF16, 2)
        oT, BoT = sb("oT", [128, 4, GT], BF16)
        relus = sb("relu", [128, GT], F32, 3)
        pTs = ps("pT", [128, 8, 128], BF16, 2)
        pms = ps("pm", [128, 512], F32, 6)

        cast_holders = [Buf(f"casth{i}") for i in range(8)]
        ci = 0
        for nm, src, nr, ncol in WSPEC:
            dst, bl = WB[nm]
            for rc in range(nr // 128):
                S.add("pool", lambda e, dst=dst, src=src, rc=rc: e.dma_start(out=dst[rc * 128:(rc + 1) * 128, :], in_=src[rc * 128:(rc + 1) * 128, :]),
                      writes=[bl[rc]], dma=True, holder=cast_holders[ci % 8])
                ci += 1

        tog = [0]

        def evac_copy(out_ap, in_ap, rb, wb):
            tog[0] ^= 1
            if tog[0]:
                S.add("act", lambda e: e.copy(out=out_ap, in_=in_ap), reads=rb, writes=wb)
            else:
                S.add("dve", lambda e: e.tensor_copy(out=out_ap, in_=in_ap), reads=rb, writes=wb)

        def load_gain(i):
            g, Bg = greps.next()
            S.add("sp", lambda e, g=g, i=i: e.dma_start(out=g[:], in_=gains[i:i + 1, :].partition_broadcast(128)), writes=[Bg], dma=True)
            return g, Bg

        def load_w(nm, r0, nrows, c0, ncols):
            wb, Bw = wbufs.next()
            src, bl = WB[nm]
            S.add("sp", lambda e, wb=wb, src=src: e.dma_start(
                out=wb[:, 0:nrows // 128, 0:ncols], in_=src[r0:r0 + nrows, c0:c0 + ncols].rearrange("(kc p) n -> p kc n", p=128)),
                reads=bl[r0 // 128:(r0 + nrows) // 128], writes=[Bw], dma=True)
            return wb, Bw

        def rstd_of(src_ap, Bsrc, n, scale_div):
            ss, Bss = sss.next()
            S.add("act", lambda e: e.activation(out=junk[:, 0:n], in_=src_ap, func=AF.Square, accum_out=ss[:, 0:1]), reads=[Bsrc], writes=[Bjunk, Bss])
            S.add("act", lambda e: e.activation(out=ss[:, 0:1], in_=ss[:, 0:1], func=AF.Ln, scale=1.0 / scale_div, bias=epst[:, 0:1]), reads=[Bss, Beps], writes=[Bss])
            S.add("act", lambda e: e.activation(out=ss[:, 0:1], in_=ss[:, 0:1], func=AF.Exp, scale=-0.5), reads=[Bss], writes=[Bss])
            return ss, Bss

        def transpose_into(src, Bsrc, nkc, dstT, BdstT, tt):
            for c0 in range(0, nkc, 8):
                nn = min(8, nkc - c0)
                pT, BpT = pTs.next()
                for c in range(nn):
                    S.add("pe", lambda e, pT=pT, c=c, kc=c0 + c: e.transpose(out=pT[:, c, :], in_=src[:, kc * 128:(kc + 1) * 128], identity=idt[:]),
                          reads=[Bsrc, Bid], writes=[BpT])
                evac_copy(dstT[:, c0:c0 + nn, tt * 128:(tt + 1) * 128], pT[:, 0:nn, :], [BpT], [BdstT])

        def norm_T(xt, Bx, g, Bg, tt):
            ss, Bss = rstd_of(xt[:], Bx, D, D)
            h, Bh = hs.next()
            S.add("dve", lambda e: e.scalar_tensor_tensor(out=h[:], in0=xt[:], scalar=ss[:, 0:1], in1=g[:], op0=ALU.mult, op1=ALU.mult),
                  reads=[Bx, Bss, Bg], writes=[Bh])
            transpose_into(h, Bh, KC, actT, BactT, tt)

        def post_res(yt, By, xt, Bx, g, Bg):
            ss, Bss = rstd_of(yt[:], By, D, D)
            S.add("dve", lambda e: e.scalar_tensor_tensor(out=yt[:], in0=yt[:], scalar=ss[:, 0:1], in1=g[:], op0=ALU.mult, op1=ALU.mult),
                  reads=[By, Bss, Bg], writes=[By])
            S.add("dve", lambda e: e.tensor_tensor(out=xt[:], in0=xt[:], in1=yt[:], op=ALU.add), reads=[Bx, By], writes=[Bx])

        def mm_tok(srcT, BsrcT, nkc, wsrc):
            for ch in range(4):
                wb, Bw = load_w(wsrc, 0, nkc * 128, ch * 512, 512)
                for tt in range(TPG):
                    pm, Bpm = pms.next()
                    for kc in range(nkc):
                        S.add("pe", lambda e, pm=pm, wb=wb, kc=kc, tt=tt: e.matmul(pm[:], lhsT=srcT[:, kc, tt * 128:(tt + 1) * 128], rhs=wb[:, kc, :],
                              start=(kc == 0), stop=(kc == nkc - 1)), reads=[Bw, BsrcT], writes=[Bpm])
                    yt, By = yts[tt]
                    evac_copy(yt[:, ch * 512:(ch + 1) * 512], pm[:], [Bpm], [By])

        gm, Bgm = load_gain(5)
        for mt in range(2):
            xt, Bx = xts[mt % TPG]
            S.add("sp", lambda e, xt=xt, mt=mt: e.dma_start(out=xt[:], in_=mem[mt * 128:(mt + 1) * 128, :]), writes=[Bx], dma=True)
            ss, Bss = rstd_of(xt[:], Bx, D, D)
            h, Bh = hs.next()
            S.add("dve", lambda e, h=h, xt=xt, ss=ss: e.scalar_tensor_tensor(out=h[:], in0=xt[:], scalar=ss[:, 0:1], in1=gm[:], op0=ALU.mult, op1=ALU.mult),
                  reads=[Bx, Bss, Bgm], writes=[Bh])
            transpose_into(h, Bh, KC, aT, BaT, mt)
        wb, Bw = load_w("xk", 0, D, 0, 512)
        for hd in range(4):
            pm, Bpm = pms.next()
            for kc in range(KC):
                S.add("pe", lambda e, pm=pm, wb=wb, kc=kc, hd=hd: e.matmul(pm[:, 0:MEM], lhsT=wb[:, kc, hd * 128:(hd + 1) * 128], rhs=aT[:, kc, 0:MEM],
                      start=(kc == 0), stop=(kc == KC - 1)), reads=[Bw, BaT], writes=[Bpm])
            evac_copy(KTs[:, hd, :], pm[:, 0:MEM], [Bpm], [BKT])
        wb, Bw = load_w("xv", 0, D, 0, 512)
        S.add("dve", lambda e: e.memset(Vs[:, :, :, 128:129], 1.0), writes=[BV])
        for mt in range(2):
            pm, Bpm = pms.next()
            for kc in range(KC):
                S.add("pe", lambda e, pm=pm, wb=wb, kc=kc, mt=mt: e.matmul(pm[:], lhsT=aT[:, kc, mt * 128:(mt + 1) * 128], rhs=wb[:, kc, :],
                      start=(kc == 0), stop=(kc == KC - 1)), reads=[Bw, BaT], writes=[Bpm])
            evac_copy(Vs[:, mt, :, 0:128], pm[:].rearrange("p (h e) -> p h e", h=4), [Bpm], [BV])

        for g in range(NG):
            t0 = g * GT
            for tt in range(TPG):
                r0 = t0 + tt * 128
                xt, Bx = xts[tt]
                S.add("sp", lambda e, xt=xt, r0=r0: e.dma_start(out=xt[:], in_=x[r0:r0 + 128, :]), writes=[Bx], dma=True)
                at, Bat = attt.next(); sm, Bsm = ssum.next(); r4, Br4 = rr.next()
                u, Bu = hs.next()
                gt_, Bgt = gtt.next()
                S.add("sp", lambda e, at=at, r0=r0: e.dma_start(out=at[:], in_=att[r0:r0 + 128, :]), writes=[Bat], dma=True)
                S.add("sp", lambda e, gt_=gt_, r0=r0: e.dma_start(out=gt_[:], in_=gate[r0:r0 + 128, :]), writes=[Bgt], dma=True)
                av = at[:].rearrange("p (h q c) -> p h q c", h=4, q=3)
                smv = sm[:].rearrange("p (h c) -> p h c", h=4)
                S.add("dve", lambda e, smv=smv, av=av: e.tensor_tensor(out=smv, in0=av[:, :, 0, :], in1=av[:, :, 1, :], op=ALU.add), reads=[Bat], writes=[Bsm])
                S.add("dve", lambda e, smv=smv, av=av: e.tensor_tensor(out=smv, in0=smv, in1=av[:, :, 2, :], op=ALU.add), reads=[Bat, Bsm], writes=[Bsm])
                S.add("dve", lambda e, r4=r4, smv=smv: e.reciprocal(out=r4[:, 0:4], in_=smv[:, :, 128]), reads=[Bsm], writes=[Br4])
                for hd in range(4):
                    S.add("act", lambda e, u=u, smv=smv, r4=r4, hd=hd: e.activation(out=u[:, hd * 128:(hd + 1) * 128], in_=smv[:, hd, 0:128], func=AF.Copy, scale=r4[:, hd:hd + 1]),
                          reads=[Bsm, Br4], writes=[Bu])
                for ki, src in enumerate((mfb, gfb)):
                    fb, Bfb = fbt.next(); ts_, Bts = tsum.next(); tq, Btq = tsq.next(); r6, Br6 = rr.next()
                    S.add("sp", lambda e, fb=fb, src=src, r0=r0: e.dma_start(out=fb[:], in_=src[:, r0:r0 + 128, :].rearrange("d p n -> p d n")), writes=[Bfb], dma=True)
                    S.add("dve", lambda e, ts_=ts_, fb=fb: e.tensor_tensor(out=ts_[:], in0=fb[:, 0, :], in1=fb[:, 1, :], op=ALU.add), reads=[Bfb], writes=[Bts])
                    S.add("dve", lambda e, tq=tq, ts_=ts_: e.tensor_tensor(out=tq[:], in0=ts_[:], in1=ts_[:], op=ALU.mult), reads=[Bts], writes=[Btq])
                    S.add("dve", lambda e, r6=r6, tq=tq: e.tensor_reduce(out=r6[:, 0:6], in_=tq[:].rearrange("p (h e) -> p h e", h=6), axis=AX.X, op=ALU.add),
                          reads=[Btq], writes=[Br6])
                    S.add("act", lambda e, r6=r6: e.activation(out=r6[:, 0:6], in_=r6[:, 0:6], func=AF.Ln, scale=1.0 / 128, bias=epst[:, 0:1]), reads=[Br6, Beps], writes=[Br6])
                    S.add("act", lambda e, r6=r6: e.activation(out=r6[:, 0:6], in_=r6[:, 0:6], func=AF.Exp, scale=-0.5), reads=[Br6], writes=[Br6])
                    for hd in range(6):
                        S.add("act", lambda e, tq=tq, ts_=ts_, r6=r6, hd=hd: e.activation(out=tq[:, hd * 128:(hd + 1) * 128], in_=ts_[:, hd * 128:(hd + 1) * 128], func=AF.Copy, scale=r6[:, hd:hd + 1]),
                              reads=[Bts, Br6], writes=[Btq])
                    S.add("dve", lambda e, tq=tq, ki=ki: e.tensor_tensor(out=tq[:], in0=tq[:], in1=hnt[:, ki, :], op=ALU.mult), reads=[Btq, Bhn], writes=[Btq])
                    S.add("dve", lambda e, u=u, tq=tq, gt_=gt_, ki=ki: e.tensor_tensor(out=u[:, 512 + ki * 768:512 + (ki + 1) * 768], in0=tq[:], in1=gt_[:, ki * 768:(ki + 1) * 768], op=ALU.mult),
                          reads=[Btq, Bgt], writes=[Bu])
                transpose_into(u, Bu, KC, actT, BactT, tt)
            mm_tok(actT, BactT, KC, "out")
            gp, Bgp = load_gain(0)
            for tt in range(TPG):
                post_res(yts[tt][0], yts[tt][1], xts[tt][0], xts[tt][1], gp, Bgp)
            gp, Bgp = load_gain(1)
            for tt in range(TPG):
                norm_T(xts[tt][0], xts[tt][1], gp, Bgp, tt)
            wb, Bw = load_w("xq", 0, D, 0, 512)
            for hd in range(4):
                pm, Bpm = pms.next()
                for kc in range(KC):
                    S.add("pe", lambda e, pm=pm, wb=wb, kc=kc, hd=hd: e.matmul(pm[:, 0:GT], lhsT=wb[:, kc, hd * 128:(hd + 1) * 128], rhs=actT[:, kc, :],
                          start=(kc == 0), stop=(kc == KC - 1)), reads=[Bw, BactT], writes=[Bpm])
                evac_copy(qTs[:, hd, :], pm[:, 0:GT], [Bpm], [BqT])
            otl = [ots.next() for _ in range(TPG)]
            for hd in range(4):
                El = []
                for mt in range(2):
                    pm, Bpm = pms.next()
                    S.add("pe", lambda e, pm=pm, hd=hd, mt=mt: e.matmul(pm[:, 0:GT], lhsT=KTs[:, hd, mt * 128:(mt + 1) * 128], rhs=qTs[:, hd, :], start=True, stop=True),
                          reads=[BKT, BqT], writes=[Bpm])
                    E, BE = Es.next()
                    S.add("act", lambda e, E=E, pm=pm: e.activation(out=E[:], in_=pm[:, 0:GT], func=AF.Exp, scale=XS), reads=[Bpm], writes=[BE])
                    El.append((E, BE))
                for tt in range(TPG):
                    pm, Bpm = pms.next()
                    for mt in range(2):
                        E, BE = El[mt]
                        S.add("pe", lambda e, pm=pm, E=E, tt=tt, mt=mt, hd=hd: e.matmul(pm[:, 0:129], lhsT=E[:, tt * 128:(tt + 1) * 128], rhs=Vs[:, mt, hd, :],
                              start=(mt == 0), stop=(mt == 1)), reads=[BE, BV], writes=[Bpm])
                    r1, Br1 = rr.next()
                    ot, Bot = otl[tt]
                    S.add("dve", lambda e, r1=r1, pm=pm: e.reciprocal(out=r1[:, 0:1], in_=pm[:, 128:129]), reads=[Bpm], writes=[Br1])
                    S.add("act", lambda e, ot=ot, pm=pm, r1=r1, hd=hd: e.activation(out=ot[:, hd * 128:(hd + 1) * 128], in_=pm[:, 0:128], func=AF.Copy, scale=r1[:, 0:1]),
                          reads=[Bpm, Br1], writes=[Bot])
            for tt in range(TPG):
                transpose_into(otl[tt][0], otl[tt][1], 4, oT, BoT, tt)
            mm_tok(oT, BoT, 4, "xo")
            gp, Bgp = load_gain(2)
            for tt in range(TPG):
                post_res(yts[tt][0], yts[tt][1], xts[tt][0], xts[tt][1], gp, Bgp)
            gp, Bgp = load_gain(3)
            for tt in range(TPG):
                norm_T(xts[tt][0], xts[tt][1], gp, Bgp, tt)
            for fc in range(DFF // 512):
                wb, Bw = load_w("up", 0, D, fc * 512, 512)
                for fj in range(4):
                    f = fc * 4 + fj
                    pm, Bpm = pms.next()
                    for kc in range(KC):
                        S.add("pe", lambda e, pm=pm, wb=wb, kc=kc, fj=fj: e.matmul(pm[:, 0:GT], lhsT=wb[:, kc, fj * 128:(fj + 1) * 128], rhs=actT[:, kc, :],
                              start=(kc == 0), stop=(kc == KC - 1)), reads=[Bw, BactT], writes=[Bpm])
                    rl, Brl = relus.next()
                    S.add("act", lambda e, rl=rl, pm=pm: e.activation(out=rl[:], in_=pm[:, 0:GT], func=AF.Relu), reads=[Bpm], writes=[Brl])
                    S.add("dve", lambda e, rl=rl, f=f: e.tensor_tensor(out=aT[:, f, :], in0=rl[:], in1=rl[:], op=ALU.mult), reads=[Brl], writes=[BaT])
            for dq in range(4):
                pol = [pms.next() for _ in range(TPG)]
                for f8 in range(DFF // 1024):
                    wb, Bw = load_w("down", f8 * 1024, 1024, dq * 512, 512)
                    for fi in range(8):
                        f = f8 * 8 + fi
                        for tt in range(TPG):
                            pm, Bpm = pol[tt]
                            S.add("pe", lambda e, pm=pm, wb=wb, fi=fi, f=f, tt=tt: e.matmul(pm[:], lhsT=aT[:, f, tt * 128:(tt + 1) * 128], rhs=wb[:, fi, :],
                                  start=(f == 0), stop=(f == DFF // 128 - 1)), reads=[Bw, BaT], writes=[Bpm])
                for tt in range(TPG):
                    pm, Bpm = pol[tt]
                    yt, By = yts[tt]
                    evac_copy(yt[:, dq * 512:(dq + 1) * 512], pm[:], [Bpm], [By])
            gp, Bgp = load_gain(4)
            for tt in range(TPG):
                post_res(yts[tt][0], yts[tt][1], xts[tt][0], xts[tt][1], gp, Bgp)
                r0 = t0 + tt * 128
                S.add("sp", lambda e, xt=xts[tt][0], r0=r0: e.dma_start(out=xo[r0:r0 + 128, :], in_=xt[:]), reads=[xts[tt][1]], dma=True)
        S.emit(st)
    return nc, S

OFF = {}
_names = ["aq","ak","av","mq","mk","mv","mo","mi","mf","gq","gk","gv","gg","gz"]
_sizes = [512,512,512,384,384,768,768,12,12,384,384,768,768,32]
_o = 0
for _n, _s in zip(_names, _sizes):
    OFF[_n] = (_o, _o + _s); _o += _s
def _r(n): return np.arange(*OFF[n])
FM_COLS = np.concatenate([_r("aq"), _r("ak")] +
    [np.concatenate([_r("mq")[h*64:(h+1)*64], _r("mk")[h*64:(h+1)*64]]) for h in range(6)] +
    [np.concatenate([_r("gq")[h*64:(h+1)*64], _r("gk")[h*64:(h+1)*64]]) for h in range(6)])
TM_COLS = np.concatenate([_r("av"), _r("mv"), _r("gv"), _r("mo"), _r("gg"), _r("mi"), _r("mf")])
GZ_COLS = _r("gz")
SW_COLS = np.concatenate([np.concatenate([_r(n)[h*128+64:(h+1)*128], _r(n)[h*128:h*128+64]]) for n in ("aq", "ak") for h in range(4)])

def rope_tables(pos):
    half = 64
    inv = (np.float32(10000.0) ** (-np.arange(half, dtype=np.float32) / np.float32(half))).astype(np.float32)
    ang = pos.astype(np.float32)[:, None] * inv[None, :]
    cos = np.cos(ang).astype(np.float32).T
    sin = np.sin(ang).astype(np.float32).T
    cosT = np.concatenate([cos, cos], 0)
    sinT = np.concatenate([-sin, sin], 0)
    return np.ascontiguousarray(cosT), np.ascontiguousarray(sinT)

def a_inputs(l, x_shard, pos, P):
    w_in = P["w_in"][l]
    cosT, sinT = rope_tables(pos)
    wdec = np.zeros((33, 768), np.float32)
    wdec[0:16, 0:384] = P["gla_w_decay"][l, 0]
    wdec[16:32, 384:768] = P["gla_w_decay"][l, 1]
    wdec[32, 0:384] = P["gla_b_decay"][l, 0]
    wdec[32, 384:768] = P["gla_b_decay"][l, 1]
    ifb = np.concatenate([P["mlstm_i_bias"][l].reshape(-1), P["mlstm_f_bias"][l].reshape(-1)])[None, :]
    return dict(x=np.ascontiguousarray(x_shard), g_pre=P["norm_mix_pre"][l][None, :],
                w_fm=np.ascontiguousarray(w_in[:, FM_COLS]), w_gz=np.ascontiguousarray(w_in[:, GZ_COLS]), w_sw=np.ascontiguousarray(w_in[:, SW_COLS]),
                w_tm=np.ascontiguousarray(w_in[:, TM_COLS]), ifb=np.ascontiguousarray(ifb.astype(np.float32)),
                wdec=wdec, cosT=cosT, sinT=sinT, ident=np.eye(128).astype(ml_dtypes.bfloat16))


BF = ml_dtypes.bfloat16
SEQ = 16384
NCORE = 8
NTC = SEQ // 4


def _run(nc, in_maps):
    res = run_bass_kernel_spmd(nc, in_maps, core_ids=list(range(len(in_maps))))
    return res.results


def _att_mask():
    a = np.arange(128)[:, None]; i = np.arange(128)[None, :]
    return np.concatenate([(a <= i - 64), (np.abs(a - i) <= 64), (a >= i + 64)], 1).astype(BF)


def _perm_tok(x, d):
    n = x.shape[0] // d
    return np.ascontiguousarray(x.reshape(n, d, *x.shape[1:]).swapaxes(0, 1).reshape(x.shape))


def _unperm_tok(x, d):
    n = x.shape[0] // d
    return np.ascontiguousarray(x.reshape(d, n, *x.shape[1:]).swapaxes(0, 1).reshape(x.shape))


def _gather_batch(A_out, b):
    FM = np.concatenate([A_out[b * 4 + s]["fm"] for s in range(4)], axis=1)
    TM = np.concatenate([A_out[b * 4 + s]["tm"] for s in range(4)], axis=0)
    GA = np.concatenate([A_out[b * 4 + s]["gates"] for s in range(4)], axis=0)
    GS = np.concatenate([A_out[b * 4 + s]["gsp"] for s in range(4)], axis=0)
    return FM, TM, GA, GS


def _b1_inputs(batches):
    in_maps = []
    mask = _att_mask()
    for c in range(NCORE):
        b, k = c // 4, c % 4
        FM, TM, GA, GS = batches[b]
        qT = FM[k * 128:(k + 1) * 128]; kT = FM[512 + k * 128:512 + (k + 1) * 128]; v = TM[:, k * 128:(k + 1) * 128]
        in_maps.append(dict(
            qT=np.stack([_perm_tok(np.ascontiguousarray(qT.T), d).T for d in DILS]),
            kT=np.stack([_perm_tok(np.ascontiguousarray(kT.T), d).T for d in DILS]),
            v=np.stack([_perm_tok(np.ascontiguousarray(v), d) for d in DILS]), mask=mask))
        in_maps[-1]["qT"] = np.ascontiguousarray(in_maps[-1]["qT"]); in_maps[-1]["kT"] = np.ascontiguousarray(in_maps[-1]["kT"])
    return in_maps


def _b2_inputs(l, P, batches):
    in_maps = []
    NCH = SEQ // 128
    for c in range(NCORE):
        b, k = c // 4, c % 4
        FM, TM, GA, GS = batches[b]
        mqk = np.zeros((3, 2, 64, SEQ + 4), BF); cw = np.zeros((3, 64, 10), np.float32); cb = np.zeros((3, 64, 2), np.float32)
        nlf = np.zeros((3, 128, NCH), np.float32); ei = np.zeros((3, 128, NCH), np.float32)
        mv = np.zeros((3, SEQ, 128), BF); gqk = np.zeros((3, 2, 64, SEQ), BF)
        gsp = np.zeros((3, SEQ, 64), np.float32); gv = np.zeros((3, SEQ, 128), BF)
        for u in range(3):
            idx = 3 * k + u
            h, d = idx // 2, idx % 2
            fl = (lambda a, ax: np.flip(a, axis=ax)) if d else (lambda a, ax: a)
            r0 = 1024 + h * 128
            mqk[u, 0, :, 2:-2] = fl(FM[r0:r0 + 64], 1); mqk[u, 1, :, 2:-2] = fl(FM[r0 + 64:r0 + 128], 1)
            cwl = P["mlstm_conv_w"][l]
            cw[u, :, 0:5] = fl(cwl[:, h * 64:(h + 1) * 64], 0).T; cw[u, :, 5:10] = fl(cwl[:, 384 + h * 64:384 + (h + 1) * 64], 0).T
            cb[u, :, 0] = P["mlstm_conv_b"][l][h * 64:(h + 1) * 64]; cb[u, :, 1] = P["mlstm_conv_b"][l][384 + h * 64:384 + (h + 1) * 64]
            nlf[u] = fl(GA[:, d * 6 + h], 0).reshape(NCH, 128).T
            ei[u] = fl(GA[:, 12 + d * 6 + h], 0).reshape(NCH, 128).T
            mv[u] = fl(TM[:, 512 + h * 128:512 + (h + 1) * 128], 0)
            r0 = 1792 + h * 128
            gqk[u, 0] = fl(FM[r0:r0 + 64], 1); gqk[u, 1] = fl(FM[r0 + 64:r0 + 128], 1)
            gsp[u] = fl(GS[:, d * 384 + h * 64:d * 384 + (h + 1) * 64], 0)
            gv[u] = fl(TM[:, 1280 + h * 128:1280 + (h + 1) * 128], 0)
        in_maps.append(dict(mqk=mqk, cw=cw, cb=cb, nlf=nlf, ei=ei, mv=mv, gqk=gqk, gsp=gsp, gv=gv,
                            triU=np.triu(np.ones((128, 128), np.float32)), ident=np.eye(128).astype(BF)))
    return in_maps


def _c_inputs(l, P, x, batches, B1_out, B2_out):
    gains = np.ascontiguousarray(np.stack([P["norm_mix_post"][l], P["norm_xatt_pre"][l], P["norm_xatt_post"][l],
                                           P["norm_mlp_pre"][l], P["norm_mlp_post"][l], P["norm_mem"][l]]))
    hn = np.ascontiguousarray(np.stack([P["mlstm_norm"][l], P["gla_norm"][l]]))
    ident = np.eye(128).astype(BF)
    per_b = []
    for b in range(2):
        att = np.zeros((SEQ, 4, 3, 129), np.float32)
        mfb = np.zeros((2, SEQ, 768), np.float32); gfb = np.zeros((2, SEQ, 768), np.float32)
        for k in range(4):
            c = b * 4 + k
            for p, d in enumerate(DILS):
                att[:, k, p, :] = _unperm_tok(B1_out[c]["o"][p], d)
            for u in range(3):
                idx = 3 * k + u
                h, d = idx // 2, idx % 2
                ym = B2_out[c]["ym"][u]; yg = B2_out[c]["yg"][u]
                if d:
                    ym = ym[::-1]; yg = yg[::-1]
                mfb[d, :, h * 128:(h + 1) * 128] = ym
                gfb[d, :, h * 128:(h + 1) * 128] = yg
        per_b.append((att.reshape(SEQ, 12 * 129), mfb, gfb))
    in_maps = []
    for c in range(NCORE):
        b, s = c // 4, c % 4
        att, mfb, gfb = per_b[b]
        sl = slice(s * NTC, (s + 1) * NTC)
        TM = batches[b][1]
        in_maps.append(dict(x=np.ascontiguousarray(x[b, sl]), att=np.ascontiguousarray(att[sl]), mfb=np.ascontiguousarray(mfb[:, sl]),
                            gfb=np.ascontiguousarray(gfb[:, sl]), gate=np.ascontiguousarray(TM[sl, 2048:3584]), hn=hn, gains=gains,
                            mem=np.ascontiguousarray(P["mem"][b]), w_out=P["w_out"][l], w_xq=P["w_xq"][l], w_xk=P["w_xk"][l],
                            w_xv=P["w_xv"][l], w_xo=P["w_xo"][l], w_up=P["w_up"][l], w_down=P["w_down"][l], ident=ident))
    return in_maps


def kernel(**inputs):
    P = {k: np.asarray(v) for k, v in inputs.items()}
    x = np.asarray(P["x"], dtype=np.float32)
    ncA, _ = build_A(NTC, 1024)
    ncB1, _ = build_B1(SEQ)
    ncB2, _ = build_B2(SEQ, 3)
    ncC, _ = build_C(NTC, 256)
    for l in range(2):
        a_maps = []
        for c in range(NCORE):
            b, s = c // 4, c % 4
            a_maps.append(a_inputs(l, x[b, s * NTC:(s + 1) * NTC], np.arange(s * NTC, (s + 1) * NTC), P))
        A_out = _run(ncA, a_maps)
        batches = [_gather_batch(A_out, b) for b in range(2)]
        del A_out
        B1_out = _run(ncB1, _b1_inputs(batches))
        B2_out = _run(ncB2, _b2_inputs(l, P, batches))
        C_out = _run(ncC, _c_inputs(l, P, x, batches, B1_out, B2_out))
        x = np.stack([np.concatenate([C_out[b * 4 + s]["xo"] for s in range(4)], axis=0) for b in range(2)]).astype(np.float32)
    return x
```

```python
from contextlib import ExitStack
import ml_dtypes
from concourse.bass_utils import run_bass_kernel_spmd
import numpy as np
import concourse.bass as bass
import concourse.mybir as mybir

F32 = mybir.dt.float32
BF16 = mybir.dt.bfloat16
ALU = mybir.AluOpType
AF = mybir.ActivationFunctionType
AX = mybir.AxisListType

SAME_ENGINE_SYNC = True


class Buf:
    __slots__ = ("name", "last_w", "readers", "sem", "semcnt", "last_dma")

    def __init__(self, name):
        self.name = name
        self.last_w = None
        self.readers = []
        self.sem = None
        self.semcnt = 0
        self.last_dma = None


class Op:
    __slots__ = ("eng", "fn", "deps", "is_dma", "holder", "sig", "dmacnt", "idx")


class Sched:
    ENGS = ("pe", "act", "dve", "pool", "sp")

    def __init__(self, nc):
        self.nc = nc
        self.ops = []

    def add(self, eng, fn, reads=(), writes=(), dma=False, holder=None):
        op = Op()
        op.eng = eng; op.fn = fn; op.is_dma = dma; op.idx = len(self.ops)
        op.sig = 0; op.dmacnt = 0
        deps = set()
        for b in reads:
            if b.last_w is not None:
                deps.add(b.last_w)
        for b in writes:
            if b.last_w is not None:
                deps.add(b.last_w)
            deps.update(b.readers)
        if dma:
            if holder is None:
                holder = writes[0] if writes else reads[0]
            op.holder = holder
            if holder.last_dma is not None:
                deps.add(holder.last_dma)
            holder.last_dma = op.idx
            holder.semcnt += 16
            op.dmacnt = holder.semcnt
        else:
            op.holder = None
        deps.discard(op.idx)
        op.deps = deps
        for b in reads:
            b.readers.append(op.idx)
        for b in writes:
            b.last_w = op.idx
            b.readers = []
        self.ops.append(op)
        return op

    def emit(self, stack):
        nc = self.nc
        ops = self.ops
        need_sig = [False] * len(ops)
        for op in ops:
            best = {}
            for d in op.deps:
                p = ops[d]
                if p.is_dma:
                    continue
                if p.eng == op.eng and (p.eng == "pe" or not SAME_ENGINE_SYNC) and not op.is_dma:
                    continue
                if p.eng not in best or best[p.eng] < d:
                    best[p.eng] = d
            op.deps = set(d for d in op.deps if ops[d].is_dma) | set(best.values())
            for d in best.values():
                need_sig[d] = True
        esem = {}
        for e in self.ENGS:
            esem[e] = stack.enter_context(nc.semaphore("s_" + e))
        ecnt = {e: 0 for e in self.ENGS}
        for i, op in enumerate(ops):
            if op.is_dma:
                if op.holder.sem is None:
                    op.holder.sem = stack.enter_context(nc.semaphore("d_" + op.holder.name))
            elif need_sig[i]:
                ecnt[op.eng] += 1
                op.sig = ecnt[op.eng]
        streams = {e: [] for e in self.ENGS}
        waited = {e: {} for e in self.ENGS}
        for i, op in enumerate(ops):
            waits = {}
            for d in op.deps:
                p = ops[d]
                if p.is_dma:
                    s, v = p.holder.sem, p.dmacnt
                else:
                    if p.eng == op.eng and (p.eng == "pe" or not SAME_ENGINE_SYNC) and not op.is_dma:
                        continue
                    s, v = esem[p.eng], p.sig
                key = id(s)
                if key not in waits or waits[key][1] < v:
                    waits[key] = (s, v)
            wl = []
            w = waited[op.eng]
            for key, (s, v) in waits.items():
                if w.get(key, 0) >= v:
                    continue
                w[key] = v
                wl.append((s, v))
            streams[op.eng].append((wl, op))
        holders = {}
        for op in ops:
            if op.is_dma:
                holders[id(op.holder)] = op.holder
        fin = []
        for hd in holders.values():
            if waited["sp"].get(id(hd.sem), 0) < hd.semcnt:
                fin.append((hd.sem, hd.semcnt))
        fop = Op(); fop.fn = None; fop.is_dma = False; fop.sig = 0
        streams["sp"].append((fin, fop))
        block = stack.enter_context(nc.Block())

        def body_for(e):
            def body(eng):
                for wl, op in streams[e]:
                    for s, v in wl:
                        eng.wait_ge(s, v)
                    if op.fn is None:
                        continue
                    ins = op.fn(eng)
                    if op.is_dma:
                        ins.then_inc(op.holder.sem, 16)
                    elif op.sig:
                        ins.then_inc(esem[e], 1)
            return body
        block.tensor(body_for("pe"))
        block.scalar(body_for("act"))
        block.vector(body_for("dve"))
        block.gpsimd(body_for("pool"))
        block.sync(body_for("sp"))
        self.stats = {e: len(streams[e]) for e in self.ENGS}
        self.semmax = dict(ecnt)


D = 2048
KC = D // 128
NFM = 2560
NTM = 3608
NTMB = 3584
EPS = 1e-6


class Rot:
    def __init__(self, items):
        self.items = items
        self.i = 0

    def next(self):
        it = self.items[self.i % len(self.items)]
        self.i += 1
        return it


def build_A(NT, GT=1024, phases=("fm", "gz", "tm")):
    nc = bass.Bass("TRN2", target_bir_lowering=False)
    dram = lambda n, s, dt, k: nc.dram_tensor(n, s, dt, kind=k).ap()
    x = dram("x", [NT, D], F32, "ExternalInput")
    g_pre = dram("g_pre", [1, D], F32, "ExternalInput")
    w_fm = dram("w_fm", [D, NFM], F32, "ExternalInput")
    w_sw = dram("w_sw", [D, 1024], F32, "ExternalInput")
    w_gz = dram("w_gz", [D, 32], F32, "ExternalInput")
    w_tm = dram("w_tm", [D, NTM], F32, "ExternalInput")
    ifb = dram("ifb", [1, 24], F32, "ExternalInput")
    wdec = dram("wdec", [33, 768], F32, "ExternalInput")
    cosT = dram("cosT", [128, NT], F32, "ExternalInput")
    sinT = dram("sinT", [128, NT], F32, "ExternalInput")
    ident = dram("ident", [128, 128], BF16, "ExternalInput")
    fm = dram("fm", [NFM, NT], BF16, "ExternalOutput")
    tm = dram("tm", [NT, NTMB], BF16, "ExternalOutput")
    gates = dram("gates", [NT, 24], F32, "ExternalOutput")
    gsp = dram("gsp", [NT, 768], F32, "ExternalOutput")

    S = Sched(nc)
    NG = NT // GT
    TPG = GT // 128
    with ExitStack() as st:
        def sb(name, shape, dt, n=1):
            out = []
            for i in range(n):
                t = st.enter_context(nc.sbuf_tensor(f"{name}{i}", shape, dt))
                out.append((t, Buf(f"{name}{i}")))
            return out[0] if n == 1 else Rot(out)

        def ps(name, shape, dt, n=1):
            out = []
            for i in range(n):
                t = st.enter_context(nc.psum_tensor(f"{name}{i}", shape, dt))
                out.append((t, Buf(f"{name}{i}")))
            return out[0] if n == 1 else Rot(out)

        grep_, Bg = sb("grep", [128, D], F32)
        idt, Bid = sb("idt", [128, 128], BF16)
        csts = sb("cst", [128, GT], F32, 2)
        snts = sb("snt", [128, GT], F32, 2)
        ifbt, Bifb = sb("ifbt", [128, 24], F32)
        wdt, Bwd = sb("wdt", [33, 768], F32)
        epst, Beps = sb("epst", [128, 1], F32)
        onet, Bone = sb("onet", [128, 1], F32)
        S.add("sp", lambda e: e.dma_start(out=grep_[:], in_=g_pre.partition_broadcast(128)), writes=[Bg], dma=True)
        S.add("sp", lambda e: e.dma_start(out=idt[:], in_=ident), writes=[Bid], dma=True)
        S.add("sp", lambda e: e.dma_start(out=ifbt[:], in_=ifb.partition_broadcast(128)), writes=[Bifb], dma=True)
        S.add("sp", lambda e: e.dma_start(out=wdt[:], in_=wdec), writes=[Bwd], dma=True)
        S.add("dve", lambda e: e.memset(epst[:], EPS), writes=[Beps])
        S.add("dve", lambda e: e.memset(onet[:], 1.0), writes=[Bone])

        xts = sb("xt", [128, D], F32, 2)
        junk, Bjunk = sb("junk", [128, D], BF16)
        sss = sb("ss", [128, 1], F32, 2)
        rstds = sb("rstd", [128, 1], F32, 2)
        hs = sb("h", [128, D], BF16, 2)
        hT, BhT = sb("hT", [128, KC, GT], BF16)
        wbufs = sb("wb", [128, KC, 512], BF16, 4)
        pTs = ps("pT", [128, 8, 128], BF16, 2)
        pms = ps("pm", [128, 512], F32, 4)
        fstage = sb("fst", [128, 512], BF16, 3)
        sws = sb("sw", [128, 512], F32, 2)
        t1s = sb("t1", [128, 512], F32, 2)
        gzT, BgzT = sb("gzT", [33, GT], F32)
        S.add("dve", lambda e: e.memset(gzT[32:33, :], 1.0), writes=[BgzT])
        gstage = sb("gst", [128, 24], F32, 2)
        g2 = sb("g2", [128, 24], F32, 2)
        zst = sb("zst", [128, 768], F32, 2)
        zst2 = sb("zst2", [128, 768], F32, 2)

        evac_tog = [0]

        def evac_copy(out_ap, in_ap, rb, wb):
            evac_tog[0] ^= 1
            if evac_tog[0]:
                S.add("act", lambda e: e.copy(out=out_ap, in_=in_ap), reads=rb, writes=wb)
            else:
                S.add("dve", lambda e: e.tensor_copy(out=out_ap, in_=in_ap), reads=rb, writes=wb)

        for g in range(NG):
            t0 = g * GT
            cst, Bcs = csts.next(); snt, Bsn = snts.next()
            S.add("sp", lambda e, cst=cst, t0=t0: e.dma_start(out=cst[:], in_=cosT[:, t0:t0 + GT]), writes=[Bcs], dma=True)
            S.add("sp", lambda e, snt=snt, t0=t0: e.dma_start(out=snt[:], in_=sinT[:, t0:t0 + GT]), writes=[Bsn], dma=True)
            for tt in range(TPG):
                r0 = t0 + tt * 128
                xt, Bx = xts.next(); ss, Bss = sss.next(); rstd, Brs = rstds.next(); h, Bh = hs.next()
                S.add("sp", lambda e, xt=xt, r0=r0: e.dma_start(out=xt[:], in_=x[r0:r0 + 128, :]), writes=[Bx], dma=True)
                S.add("act", lambda e, xt=xt, ss=ss: e.activation(out=junk[:], in_=xt[:], func=AF.Square, accum_out=ss[:]),
                      reads=[Bx], writes=[Bjunk, Bss])
                S.add("act", lambda e, ss=ss, rstd=rstd: e.activation(out=rstd[:], in_=ss[:], func=AF.Ln, scale=1.0 / D, bias=epst[:, 0:1]),
                      reads=[Bss, Beps], writes=[Brs])
                S.add("act", lambda e, rstd=rstd: e.activation(out=rstd[:], in_=rstd[:], func=AF.Exp, scale=-0.5),
                      reads=[Brs], writes=[Brs])
                S.add("dve", lambda e, h=h, xt=xt, rstd=rstd: e.scalar_tensor_tensor(out=h[:], in0=xt[:], scalar=rstd[:, 0:1], in1=grep_[:], op0=ALU.mult, op1=ALU.mult),
                      reads=[Bx, Brs, Bg], writes=[Bh])
                for half in range(2):
                    pT, BpT = pTs.next()
                    for c in range(8):
                        kc = half * 8 + c
                        S.add("pe", lambda e, pT=pT, c=c, h=h, kc=kc: e.transpose(out=pT[:, c, :], in_=h[:, kc * 128:(kc + 1) * 128], identity=idt[:]),
                              reads=[Bh, Bid], writes=[BpT])
                    evac_copy(hT[:, half * 8:half * 8 + 8, tt * 128:(tt + 1) * 128], pT[:], [BpT], [BhT])

            def load_w(src, c0, ncols):
                wb, Bw = wbufs.next()
                S.add("pool", lambda e, wb=wb, src=src, c0=c0, ncols=ncols: e.dma_start(
                    out=wb[:, :, 0:ncols], in_=src[:, c0:c0 + ncols].rearrange("(kc p) n -> p kc n", p=128)),
                    writes=[Bw], dma=True)
                return wb, Bw

            for ch in range(NFM // 512 if 'fm' in phases else 0):
                wb, Bw = load_w(w_fm, ch * 512, 512)
                if ch < 2:
                    wb2, Bw2 = load_w(w_sw, ch * 512, 512)
                for jj in range(4):
                    j = ch * 4 + jj
                    for s in range(GT // 512):
                        tk = s * 512
                        pm, Bpm = pms.next()
                        for kc in range(KC):
                            S.add("pe", lambda e, pm=pm, wb=wb, jj=jj, kc=kc, tk=tk: e.matmul(
                                pm[:], lhsT=wb[:, kc, jj * 128:(jj + 1) * 128], rhs=hT[:, kc, tk:tk + 512],
                                start=(kc == 0), stop=(kc == KC - 1)), reads=[Bw, BhT], writes=[Bpm])
                        fs, Bfs = fstage.next()
                        if j < 8:
                            pm2, Bpm2 = pms.next()
                            for kc in range(KC):
                                S.add("pe", lambda e, pm2=pm2, wb2=wb2, jj=jj, kc=kc, tk=tk: e.matmul(
                                    pm2[:], lhsT=wb2[:, kc, jj * 128:(jj + 1) * 128], rhs=hT[:, kc, tk:tk + 512],
                                    start=(kc == 0), stop=(kc == KC - 1)), reads=[Bw2, BhT], writes=[Bpm2])
                            sw, Bsw = sws.next(); t1, Bt1 = t1s.next()
                            c0 = tk
                            S.add("dve", lambda e, t1=t1, pm=pm, c0=c0, cst=cst: e.tensor_tensor(out=t1[:], in0=pm[:], in1=cst[:, c0:c0 + 512], op=ALU.mult),
                                  reads=[Bpm, Bcs], writes=[Bt1])
                            S.add("dve", lambda e, sw=sw, pm2=pm2, c0=c0, snt=snt: e.tensor_tensor(out=sw[:], in0=pm2[:], in1=snt[:, c0:c0 + 512], op=ALU.mult),
                                  reads=[Bpm2, Bsn], writes=[Bsw])
                            S.add("dve", lambda e, fs=fs, t1=t1, sw=sw: e.tensor_tensor(out=fs[:], in0=t1[:], in1=sw[:], op=ALU.add),
                                  reads=[Bt1, Bsw], writes=[Bfs])
                        else:
                            evac_copy(fs[:], pm[:], [Bpm], [Bfs])
                        S.add("sp", lambda e, fs=fs, j=j, c0=t0 + tk: e.dma_start(out=fm[j * 128:(j + 1) * 128, c0:c0 + 512], in_=fs[:]),
                              reads=[Bfs], dma=True)
            if 'gz' in phases:
                wb, Bw = load_w(w_gz, 0, 32)
            for s in range(GT // 512 if 'gz' in phases else 0):
                tk = s * 512
                pm, Bpm = pms.next()
                for kc in range(KC):
                    S.add("pe", lambda e, pm=pm, wb=wb, kc=kc, tk=tk: e.matmul(
                        pm[0:32, :], lhsT=wb[:, kc, 0:32], rhs=hT[:, kc, tk:tk + 512],
                        start=(kc == 0), stop=(kc == KC - 1)), reads=[Bw, BhT], writes=[Bpm])
                evac_copy(gzT[0:32, tk:tk + 512], pm[0:32, :], [Bpm], [BgzT])
            for tt in range(TPG if 'gz' in phases else 0):
                zs, Bzs = zst.next(); zs2, Bzs2 = zst2.next()
                for hh in range(2):
                    pm, Bpm = pms.next()
                    S.add("pe", lambda e, pm=pm, tt=tt, hh=hh: e.matmul(
                        pm[:, 0:384], lhsT=gzT[0:33, tt * 128:(tt + 1) * 128], rhs=wdt[0:33, hh * 384:(hh + 1) * 384],
                        start=True, stop=True), reads=[BgzT, Bwd], writes=[Bpm])
                    S.add("act", lambda e, zs=zs, pm=pm, hh=hh: e.activation(out=zs[:, hh * 384:(hh + 1) * 384], in_=pm[:, 0:384], func=AF.Exp, scale=-1.0),
                          reads=[Bpm], writes=[Bzs])
                S.add("act", lambda e, zs=zs, zs2=zs2: e.activation(out=zs2[:], in_=zs[:], func=AF.Ln, bias=onet[:, 0:1]),
                      reads=[Bzs, Bone], writes=[Bzs2])
                r0 = t0 + tt * 128
                S.add("sp", lambda e, zs2=zs2, r0=r0: e.dma_start(out=gsp[r0:r0 + 128, :], in_=zs2[:]), reads=[Bzs2], dma=True)
            for ch in range(8 if 'tm' in phases else 0):
                ncols = 512 if ch < 7 else 24
                wb, Bw = load_w(w_tm, ch * 512, ncols)
                for tt in range(TPG):
                    r0 = t0 + tt * 128
                    pm, Bpm = pms.next()
                    for kc in range(KC):
                        S.add("pe", lambda e, pm=pm, wb=wb, kc=kc, tt=tt, ncols=ncols: e.matmul(
                            pm[:, 0:ncols], lhsT=hT[:, kc, tt * 128:(tt + 1) * 128], rhs=wb[:, kc, 0:ncols],
                            start=(kc == 0), stop=(kc == KC - 1)), reads=[Bw, BhT], writes=[Bpm])
                    if ch < 7:
                        fs, Bfs = fstage.next()
                        lo = ch * 512
                        segs = []
                        for (a, b, fn) in ((0, 2048, None), (2048, 2816, AF.Sigmoid), (2816, 3584, AF.Silu)):
                            a2, b2 = max(a, lo), min(b, lo + 512)
                            if a2 < b2:
                                segs.append((a2 - lo, b2 - lo, fn))
                        for (a, b, fn) in segs:
                            if fn is None:
                                evac_copy(fs[:, a:b], pm[:, a:b], [Bpm], [Bfs])
                            else:
                                S.add("act", lambda e, fs=fs, pm=pm, a=a, b=b, fn=fn: e.activation(out=fs[:, a:b], in_=pm[:, a:b], func=fn),
                                      reads=[Bpm], writes=[Bfs])
                        S.add("sp", lambda e, fs=fs, r0=r0, lo=lo: e.dma_start(out=tm[r0:r0 + 128, lo:lo + 512], in_=fs[:]),
                              reads=[Bfs], dma=True)
                    else:
                        gs, Bgs = gstage.next(); gq, Bgq = g2.next()
                        S.add("dve", lambda e, gs=gs, pm=pm: e.tensor_tensor(out=gs[:], in0=pm[:, 0:24], in1=ifbt[:], op=ALU.add),
                              reads=[Bpm, Bifb], writes=[Bgs])
                        S.add("act", lambda e, gs=gs, gq=gq: e.activation(out=gq[:, 12:24], in_=gs[:, 0:12], func=AF.Exp),
                              reads=[Bgs], writes=[Bgq])
                        S.add("act", lambda e, gs=gs: e.activation(out=gs[:, 12:24], in_=gs[:, 12:24], func=AF.Exp, scale=-1.0),
                              reads=[Bgs], writes=[Bgs])
                        S.add("act", lambda e, gs=gs, gq=gq: e.activation(out=gq[:, 0:12], in_=gs[:, 12:24], func=AF.Ln, bias=onet[:, 0:1]),
                              reads=[Bgs, Bone], writes=[Bgq])
                        S.add("sp", lambda e, gq=gq, r0=r0: e.dma_start(out=gates[r0:r0 + 128, :], in_=gq[:]), reads=[Bgq], dma=True)
        S.emit(st)
    return nc, S


DILS = (1, 4, 16)


def build_B1(S_):
    nc = bass.Bass("TRN2", target_bir_lowering=False)
    dram = lambda n, s, dt, k: nc.dram_tensor(n, s, dt, kind=k).ap()
    NTL = S_ // 128
    qTd = dram("qT", [3, 128, S_], BF16, "ExternalInput")
    kTd = dram("kT", [3, 128, S_], BF16, "ExternalInput")
    vd = dram("v", [3, S_, 128], BF16, "ExternalInput")
    maskd = dram("mask", [128, 384], BF16, "ExternalInput")
    od = dram("o", [3, S_, 129], F32, "ExternalOutput")
    SCALE = 128.0 ** -0.5
    S = Sched(nc)
    with ExitStack() as st:
        def sb(name, shape, dt, n=1):
            out = []
            for i in range(n):
                t = st.enter_context(nc.sbuf_tensor(f"{name}{i}", shape, dt))
                out.append((t, Buf(f"{name}{i}")))
            return out[0] if n == 1 else Rot(out)

        def ps(name, shape, dt, n=1):
            out = []
            for i in range(n):
                t = st.enter_context(nc.psum_tensor(f"{name}{i}", shape, dt))
                out.append((t, Buf(f"{name}{i}")))
            return out[0] if n == 1 else Rot(out)

        mask, Bmask = sb("mask", [128, 384], BF16)
        S.add("sp", lambda e: e.dma_start(out=mask[:], in_=maskd), writes=[Bmask], dma=True)
        qT, BqT = sb("qT", [128, S_], BF16)
        kT, BkT = sb("kT", [128, S_], BF16)
        NVQ = 4
        vq = [sb(f"vq{i}", [128, NTL // NVQ, 129], BF16) for i in range(NVQ)]
        for i in range(NVQ):
            S.add("dve", lambda e, i=i: e.memset(vq[i][0][:, :, 128:129], 1.0), writes=[vq[i][1]])
        Es = sb("E", [128, 384], BF16, 3)
        Ems = sb("Em", [128, 384], BF16, 3)
        ost = sb("ost", [128, 8, 129], F32, 2)
        pS = ps("pS", [128, 512], F32, 2)
        pO = [ps(f"pO{i}", [128, 512], F32) for i in range(4)]
        tog = [0]
        for p, d in enumerate(DILS):
            n = S_ // d
            T = n // 128
            S.add("sp", lambda e, p=p: e.dma_start(out=qT[:], in_=qTd[p]), writes=[BqT], dma=True)
            S.add("sp", lambda e, p=p: e.dma_start(out=kT[:], in_=kTd[p]), writes=[BkT], dma=True)
            per = NTL // NVQ
            for i in range(NVQ):
                S.add("sp", lambda e, p=p, i=i: e.dma_start(out=vq[i][0][:, :, 0:128],
                      in_=vd[p, i * per * 128:(i + 1) * per * 128, :].rearrange("(j a) e -> a j e", a=128)), writes=[vq[i][1]], dma=True)
            st_ = {"os": None, "Bos": None}
            tiles = [(seg, j) for seg in range(d) for j in range(T)]
            pend = {}

            def sA(seg, j):
                g = seg * T + j
                jl, jh = max(j - 1, 0), min(j + 1, T - 1)
                q0 = (seg * T + jl) * 128
                nq = (jh - jl + 1) * 128
                mc0 = (jl - (j - 1)) * 128
                ps_, Bps = pS.next()
                S.add("pe", lambda e: e.matmul(ps_[:, 0:nq], lhsT=kT[:, g * 128:(g + 1) * 128], rhs=qT[:, q0:q0 + nq], start=True, stop=True),
                      reads=[BkT, BqT], writes=[Bps])
                E, BE = Es.next(); Em, BEm = Ems.next()
                S.add("act", lambda e: e.activation(out=E[:, 0:nq], in_=ps_[:, 0:nq], func=AF.Exp, scale=SCALE), reads=[Bps], writes=[BE])
                S.add("dve", lambda e: e.tensor_tensor(out=Em[:, 0:nq], in0=E[:, 0:nq], in1=mask[:, mc0:mc0 + nq], op=ALU.mult),
                      reads=[BE, Bmask], writes=[BEm])
                pend[(seg, j)] = (Em, BEm)

            def sB(seg, j):
                g = seg * T + j
                jl, jh = max(j - 1, 0), min(j + 1, T - 1)
                Em, BEm = pend.pop((seg, j))
                vt, Bvt = vq[g // per]
                gi = g % per
                for t in range(jl, jh + 1):
                    gt = seg * T + t
                    po, Bpo = pO[gt % 4]
                    c0 = (t - jl) * 128
                    first = (j == max(t - 1, 0)); last = (j == min(t + 1, T - 1))
                    S.add("pe", lambda e, po=po, c0=c0, first=first, last=last: e.matmul(
                        po[:, 0:129], lhsT=Em[:, c0:c0 + 128], rhs=vt[:, gi, :], start=first, stop=last), reads=[BEm, Bvt], writes=[Bpo])
                done = [j - 1] if j >= 1 else []
                if j == T - 1:
                    done.append(j)
                for t in done:
                    gt = seg * T + t
                    po, Bpo = pO[gt % 4]
                    if gt % 8 == 0:
                        st_["os"], st_["Bos"] = ost.next()
                    os_, Bos = st_["os"], st_["Bos"]
                    tog[0] ^= 1
                    if tog[0]:
                        S.add("act", lambda e, os_=os_, po=po, gt=gt: e.copy(out=os_[:, gt % 8, :], in_=po[:, 0:129]), reads=[Bpo], writes=[Bos])
                    else:
                        S.add("dve", lambda e, os_=os_, po=po, gt=gt: e.tensor_copy(out=os_[:, gt % 8, :], in_=po[:, 0:129]), reads=[Bpo], writes=[Bos])
                    if gt % 8 == 7:
                        r0 = (gt - 7) * 128
                        S.add("sp", lambda e, os_=os_, r0=r0, p=p: e.dma_start(out=od[p, r0:r0 + 1024, :].rearrange("(j a) e -> a j e", a=128), in_=os_[:]),
                              reads=[Bos], dma=True)

            def run_pattern():
                for i, (seg, j) in enumerate(tiles):
                    sA(seg, j)
                    if i >= 1:
                        sB(*tiles[i - 1])
                sB(*tiles[-1])
            run_pattern()
        S.emit(st)
    return nc, S


L = 128
BLK = 4


def build_B2(S_, NU=3):
    nc = bass.Bass("TRN2", target_bir_lowering=False)
    dram = lambda n, s, dt, k: nc.dram_tensor(n, s, dt, kind=k).ap()
    NCH = S_ // L
    NB = NCH // BLK
    BT = BLK * L
    mqk = dram("mqk", [NU, 2, 64, S_ + 4], BF16, "ExternalInput")
    cw = dram("cw", [NU, 64, 10], F32, "ExternalInput")
    cb = dram("cb", [NU, 64, 2], F32, "ExternalInput")
    nlf = dram("nlf", [NU, 128, NCH], F32, "ExternalInput")
    ei = dram("ei", [NU, 128, NCH], F32, "ExternalInput")
    mv = dram("mv", [NU, S_, 128], BF16, "ExternalInput")
    gqk = dram("gqk", [NU, 2, 64, S_], BF16, "ExternalInput")
    gsp = dram("gsp", [NU, S_, 64], F32, "ExternalInput")
    gv = dram("gv", [NU, S_, 128], BF16, "ExternalInput")
    triU = dram("triU", [128, 128], F32, "ExternalInput")
    ident = dram("ident", [128, 128], BF16, "ExternalInput")
    ym = dram("ym", [NU, S_, 128], F32, "ExternalOutput")
    yg = dram("yg", [NU, S_, 128], F32, "ExternalOutput")

    S = Sched(nc)
    with ExitStack() as st:
        def sb(name, shape, dt, n=1):
            out = []
            for i in range(n):
                t = st.enter_context(nc.sbuf_tensor(f"{name}{i}", shape, dt))
                out.append((t, Buf(f"{name}{i}")))
            return out[0] if n == 1 else Rot(out)

        def ps(name, shape, dt, n=1):
            out = []
            for i in range(n):
                t = st.enter_context(nc.psum_tensor(f"{name}{i}", shape, dt))
                out.append((t, Buf(f"{name}{i}")))
            return out[0] if n == 1 else Rot(out)

        tri, Btri = sb("tri", [128, 128], F32)
        idt, Bid = sb("idt", [128, 128], BF16)
        S.add("sp", lambda e: e.dma_start(out=tri[:], in_=triU), writes=[Btri], dma=True)
        S.add("sp", lambda e: e.dma_start(out=idt[:], in_=ident), writes=[Bid], dma=True)

        class G:
            pass
        groups = []
        for kind in ("m", "g"):
            o = G()
            o.kind = kind
            o.NV = 129 if kind == "m" else 128
            o.sc = 1.0 if kind == "m" else 1.0 / 16.0
            o.yout = ym if kind == "m" else yg
            if kind == "m":
                o.cwt, o.Bcw = sb("cwt", [64, NU, 10], F32)
                o.cbt, o.Bcb = sb("cbt", [64, NU, 2], F32)
                o.dg, o.Bdg = sb("dg", [64, NU * 10, 64], BF16)
                o.nlft, o.Bnlf = sb("nlft", [128, NU, NCH], F32)
                o.eit, o.Bei = sb("eit", [128, NU, NCH], F32)
                o.xps = [sb(f"xp{i}_", [64, NU, BT + 4], BF16, 2) for i in range(2)]
                o.vraw = sb("vraw", [128, NU, BLK, 128], BF16, 2)
            else:
                o.spb = sb("spb", [128, NU, BLK, 64], F32, 2)
            o.qks = [sb(f"qk{kind}{i}_", [64, NU, BT], BF16, 2) for i in range(2)]
            o.vpp = sb("vpp" + kind, [128, NU, BLK, 129], BF16, 2)
            o.ost = sb("ost" + kind, [128, NU, BLK, 128], F32, 2)
            o.Ct, o.BC = sb("C" + kind, [64, NU, 129], F32)
            o.Cbfs = sb("Cbf" + kind, [64, NU, 129], BF16, 2)
            o.Eqs = sb("Eq" + kind, [64, NU, 128], F32, 4)
            o.Eks = sb("Ek" + kind, [64, NU, 128], F32, 4)
            o.qss = sb("qs" + kind, [64, NU, 128], BF16, 4)
            o.kss = sb("ks" + kind, [64, NU, 128], BF16, 4)
            o.Ams = sb("Am" + kind, [128, NU, 128], BF16, 3)
            o.ksts = sb("kst" + kind, [128, NU, 64], BF16, 3)
            o.rs = sb("r" + kind, [128, NU], F32, 3)
            groups.append(o)

        pcs = ps("pc", [64, 512], F32, 1)
        pas = ps("pa", [128, 512], F32, 2)
        pos = ps("po", [128, 512], F32, 2)
        pps = ps("pp", [64, 512], F32, 1)
        pts = ps("pt", [128, 1024], BF16, 1)
        pcv = ps("pcv", [64, 512], F32, 1)
        PO = 160

        for o in groups:
            if o.kind == "m":
                S.add("sp", lambda e, o=o: e.dma_start(out=o.cwt[:], in_=cw.rearrange("u p k -> p u k")), writes=[o.Bcw], dma=True)
                S.add("sp", lambda e, o=o: e.dma_start(out=o.cbt[:], in_=cb.rearrange("u p k -> p u k")), writes=[o.Bcb], dma=True)
                S.add("sp", lambda e, o=o: e.dma_start(out=o.nlft[:], in_=nlf.rearrange("u p k -> p u k")), writes=[o.Bnlf], dma=True)
                S.add("sp", lambda e, o=o: e.dma_start(out=o.eit[:], in_=ei.rearrange("u p k -> p u k")), writes=[o.Bei], dma=True)
                for u in range(NU):
                    for j in range(10):
                        S.add("dve", lambda e, o=o, u=u, j=j: e.tensor_scalar(out=o.dg[:, u * 10 + j, :], in0=idt[0:64, 0:64], scalar1=o.cwt[:, u, j:j + 1], scalar2=None, op0=ALU.mult),
                              reads=[Bid, o.Bcw], writes=[o.Bdg])
            S.add("dve", lambda e, o=o: e.memset(o.Ct[:], 0.0), writes=[o.BC])
            o.Cbf, o.BCbf = None, None

        bs = {}
        chs = {}

        def load_block(blk):
            b0 = blk * BT
            for o in groups:
                kind = o.kind
                B_ = G()
                qk_t = []
                for i in range(2):
                    qt, Bq = o.qks[i].next()
                    if kind == "m":
                        xp, Bxp = o.xps[i].next()
                        S.add("sp", lambda e, xp=xp, i=i: e.dma_start(out=xp[:], in_=mqk[:, i, :, b0:b0 + BT + 4].rearrange("u p t -> p u t")), writes=[Bxp], dma=True)
                        for u in range(NU):
                            pv, Bpv = pcv
                            for j in range(5):
                                S.add("pe", lambda e, pv=pv, o=o, u=u, i=i, j=j, xp=xp: e.matmul(pv[:, 0:BT], lhsT=o.dg[:, u * 10 + 5 * i + j, :], rhs=xp[:, u, j:j + BT],
                                      start=(j == 0), stop=(j == 4)), reads=[o.Bdg, Bxp], writes=[Bpv])
                            S.add("act", lambda e, qt=qt, pv=pv, o=o, u=u, i=i: e.activation(out=qt[:, u, :], in_=pv[:, 0:BT], func=AF.Silu, bias=o.cbt[:, u, i:i + 1]),
                                  reads=[Bpv, o.Bcb], writes=[Bq])
                    else:
                        S.add("sp", lambda e, qt=qt, i=i: e.dma_start(out=qt[:], in_=gqk[:, i, :, b0:b0 + BT].rearrange("u p t -> p u t")), writes=[Bq], dma=True)
                    qk_t.append((qt, Bq))
                (B_.qT, B_.BqT), (B_.kT, B_.BkT) = qk_t
                vp, Bvp = o.vpp.next()
                B_.vp, B_.Bvp = vp, Bvp
                if kind == "m":
                    vr, Bvr = o.vraw.next()
                    for u in range(NU):
                        S.add("sp", lambda e, vr=vr, u=u: e.dma_start(out=vr[:, u], in_=mv[u, b0:b0 + BT, :].rearrange("(c a) e -> a c e", a=128)), writes=[Bvr], dma=True)
                    esl = o.eit[:, :, blk * BLK:(blk + 1) * BLK]
                    S.add("dve", lambda e, vp=vp, vr=vr, esl=esl: e.tensor_tensor(out=vp[:, :, :, 0:128], in0=vr[:], in1=esl.unsqueeze(3).to_broadcast([128, NU, BLK, 128]), op=ALU.mult),
                          reads=[Bvr, o.Bei], writes=[Bvp])
                    S.add("dve", lambda e, vp=vp, esl=esl: e.tensor_copy(out=vp[:, :, :, 128], in_=esl), reads=[o.Bei], writes=[Bvp])
                else:
                    for u in range(NU):
                        S.add("sp", lambda e, vp=vp, u=u: e.dma_start(out=vp[:, u, :, 0:128], in_=gv[u, b0:b0 + BT, :].rearrange("(c a) e -> a c e", a=128)), writes=[Bvp], dma=True)
                    B_.sp_, B_.Bsp = o.spb.next()
                    for u in range(NU):
                        S.add("sp", lambda e, sp_=B_.sp_, u=u: e.dma_start(out=sp_[:, u], in_=gsp[u, b0:b0 + BT, :].rearrange("(c a) e -> a c e", a=128)), writes=[B_.Bsp], dma=True)
                B_.os_, B_.Bos = o.ost.next()
                bs[(kind, blk)] = B_

        def s1(o, cg):
            kind, NV, sc = o.kind, o.NV, o.sc
            blk, c = cg // BLK, cg % BLK
            csl = slice(c * L, (c + 1) * L)
            B_ = bs[(kind, blk)]
            H = G(); chs[(kind, cg)] = H
            pc, Bpc = pcs
            for u in range(NU):
                if kind == "m":
                    S.add("pe", lambda e, pc=pc, o=o, u=u: e.matmul(pc[:, u * 128:(u + 1) * 128], lhsT=o.nlft[:, u, cg:cg + 1].to_broadcast([128, 64]), rhs=tri[:], start=True, stop=True),
                          reads=[o.Bnlf, Btri], writes=[Bpc])
                else:
                    S.add("pe", lambda e, pc=pc, sp_=B_.sp_, u=u: e.matmul(pc[:, u * 128:(u + 1) * 128], lhsT=sp_[:, u, c, :], rhs=tri[:], start=True, stop=True),
                          reads=[B_.Bsp, Btri], writes=[Bpc])
            H.Eq, H.BEq = o.Eqs.next(); Ek, BEk = o.Eks.next()
            pcv3 = pc[:, 0:NU * 128].rearrange("p (u t) -> p u t", u=NU)
            S.add("act", lambda e, Eq=H.Eq, pcv3=pcv3: e.activation(out=Eq[:], in_=pcv3, func=AF.Exp, scale=-sc), reads=[Bpc], writes=[H.BEq])
            S.add("act", lambda e, Ek=Ek, pcv3=pcv3: e.activation(out=Ek[:], in_=pcv3, func=AF.Exp, scale=sc), reads=[Bpc], writes=[BEk])
            H.qs, H.Bqs = o.qss.next(); H.ks, H.Bks = o.kss.next()
            S.add("pool", lambda e, qs=H.qs, qT=B_.qT, Eq=H.Eq: e.tensor_tensor(out=qs[:], in0=qT[:, :, csl], in1=Eq[:], op=ALU.mult), reads=[B_.BqT, H.BEq], writes=[H.Bqs])
            S.add("pool", lambda e, ks=H.ks, kT=B_.kT, Ek=Ek: e.tensor_tensor(out=ks[:], in0=kT[:, :, csl], in1=Ek[:], op=ALU.mult), reads=[B_.BkT, BEk], writes=[H.Bks])

        def s2(o, cg):
            kind = o.kind
            H = chs[(kind, cg)]
            qs, Bqs, ks, Bks = H.qs, H.Bqs, H.ks, H.Bks
            pa, Bpa = pas.next()
            for u in range(NU):
                S.add("pe", lambda e, pa=pa, u=u: e.matmul(pa[:, u * 128:(u + 1) * 128], lhsT=ks[:, u, :], rhs=qs[:, u, :], start=True, stop=True),
                      reads=[Bks, Bqs], writes=[Bpa])
            H.Am, H.BAm = o.Ams.next()
            S.add("dve", lambda e, Am=H.Am, pa=pa: e.tensor_tensor(out=Am[:], in0=pa[:, 0:NU * 128].rearrange("p (u t) -> p u t", u=NU),
                                                               in1=tri[:].unsqueeze(1).to_broadcast([128, NU, 128]), op=ALU.mult), reads=[Bpa, Btri], writes=[H.BAm])
            pt, Bpt = pts
            for u in range(NU):
                S.add("pe", lambda e, pt=pt, u=u: e.transpose(out=pt[:, u * 64:(u + 1) * 64], in_=ks[:, u, :], identity=idt[0:64, 0:64]), reads=[Bks, Bid], writes=[Bpt])
            H.kst, H.Bkst = o.ksts.next()
            S.add("act", lambda e, kst=H.kst, pt=pt: e.copy(out=kst[:], in_=pt[:, 0:NU * 64].rearrange("p (u t) -> p u t", u=NU)), reads=[Bpt], writes=[H.Bkst])

        def s3(o, cg):
            kind, NV = o.kind, o.NV
            blk, c = cg // BLK, cg % BLK
            first = (cg == 0)
            B_ = bs[(kind, blk)]
            H = chs.pop((kind, cg))
            Am, BAm, qs, Bqs, kst, Bkst, Eq, BEq = H.Am, H.BAm, H.qs, H.Bqs, H.kst, H.Bkst, H.Eq, H.BEq
            vp, Bvp = B_.vp, B_.Bvp
            Cbf_prev, BCbf_prev = o.Cbf, o.BCbf
            po, Bpo = pos.next()
            for u in range(NU):
                S.add("pe", lambda e, po=po, u=u: e.matmul(po[:, u * PO:u * PO + NV], lhsT=Am[:, u, :], rhs=vp[:, u, c, 0:NV], start=True, stop=first),
                      reads=[BAm, Bvp], writes=[Bpo])
                if not first:
                    S.add("pe", lambda e, po=po, u=u: e.matmul(po[:, u * PO:u * PO + NV], lhsT=qs[:, u, :], rhs=Cbf_prev[:, u, 0:NV], start=False, stop=True),
                          reads=[Bqs, BCbf_prev], writes=[Bpo])
            pp, Bpp = pps
            for u in range(NU):
                S.add("pe", lambda e, pp=pp, u=u: e.matmul(pp[:, u * PO:u * PO + NV], lhsT=kst[:, u, :], rhs=vp[:, u, c, 0:NV], start=True, stop=True),
                      reads=[Bkst, Bvp], writes=[Bpp])
            ppv = pp[:, 0:NU * PO].rearrange("p (u t) -> p u t", u=NU)[:, :, 0:NV]
            pov = po[:, 0:NU * PO].rearrange("p (u t) -> p u t", u=NU)
            S.add("dve", lambda e: e.tensor_tensor(out=o.Ct[:, :, 0:NV], in0=ppv, in1=o.Ct[:, :, 0:NV], op=ALU.add), reads=[Bpp, o.BC], writes=[o.BC])
            S.add("dve", lambda e: e.tensor_tensor(out=o.Ct[:, :, 0:NV], in0=o.Ct[:, :, 0:NV], in1=Eq[:, :, 127:128].to_broadcast([64, NU, NV]), op=ALU.mult),
                  reads=[o.BC, BEq], writes=[o.BC])
            o.Cbf, o.BCbf = o.Cbfs.next()
            S.add("pool", lambda e, Cbf=o.Cbf: e.tensor_copy(out=Cbf[:, :, 0:NV], in_=o.Ct[:, :, 0:NV]), reads=[o.BC], writes=[o.BCbf])
            os_, Bos = B_.os_, B_.Bos
            if kind == "m":
                r, Br = o.rs.next()
                S.add("act", lambda e: e.activation(out=r[:], in_=pov[:, :, 128], func=AF.Abs), reads=[Bpo], writes=[Br])
                S.add("dve", lambda e: e.tensor_scalar(out=r[:], in0=r[:], scalar1=8.0, scalar2=None, op0=ALU.max), reads=[Br], writes=[Br])
                S.add("dve", lambda e: e.reciprocal(out=r[:], in_=r[:]), reads=[Br], writes=[Br])
                S.add("dve", lambda e: e.tensor_tensor(out=os_[:, :, c, :], in0=pov[:, :, 0:128], in1=r[:].unsqueeze(2).to_broadcast([128, NU, 128]), op=ALU.mult),
                      reads=[Bpo, Br], writes=[Bos])
            else:
                S.add("act", lambda e: e.activation(out=os_[:, :, c, :], in_=pov[:, :, 0:128], func=AF.Copy, scale=0.125), reads=[Bpo], writes=[Bos])
            if c == BLK - 1:
                b0 = blk * BT
                for u in range(NU):
                    S.add("sp", lambda e, u=u: e.dma_start(out=o.yout[u, b0:b0 + BT, :].rearrange("(c a) e -> a c e", a=128), in_=os_[:, u]),
                          reads=[Bos], dma=True)

        for t in range(NCH + 2):
            if t < NCH and t % BLK == 0:
                load_block(t // BLK)
            for o in groups:
                if t < NCH:
                    s1(o, t)
                if 0 <= t - 1 < NCH:
                    s2(o, t - 1)
                if 0 <= t - 2 < NCH:
                    s3(o, t - 2)
        S.emit(st)
    return nc, S


D = 2048
KC = 16
DFF = 8192
EPS = 1e-6
MEM = 256


def build_C(NT, GT=256):
    nc = bass.Bass("TRN2", target_bir_lowering=False)
    dram = lambda n, s, dt, k: nc.dram_tensor(n, s, dt, kind=k).ap()
    I = "ExternalInput"
    x = dram("x", [NT, D], F32, I)
    att = dram("att", [NT, 12 * 129], F32, I)
    mfb = dram("mfb", [2, NT, 768], F32, I)
    gfb = dram("gfb", [2, NT, 768], F32, I)
    gate = dram("gate", [NT, 1536], BF16, I)
    hn = dram("hn", [2, 768], F32, I)
    gains = dram("gains", [6, D], F32, I)
    mem = dram("mem", [MEM, D], F32, I)
    w_out = dram("w_out", [D, D], F32, I)
    w_xq = dram("w_xq", [D, 512], F32, I)
    w_xk = dram("w_xk", [D, 512], F32, I)
    w_xv = dram("w_xv", [D, 512], F32, I)
    w_xo = dram("w_xo", [512, D], F32, I)
    w_up = dram("w_up", [D, DFF], F32, I)
    w_down = dram("w_down", [DFF, D], F32, I)
    ident = dram("ident", [128, 128], BF16, I)
    xo = dram("xo", [NT, D], F32, "ExternalOutput")

    WSPEC = [("xk", w_xk, D, 512), ("xv", w_xv, D, 512), ("out", w_out, D, D), ("xq", w_xq, D, 512),
             ("xo", w_xo, 512, D), ("up", w_up, D, DFF), ("down", w_down, DFF, D)]
    WB = {}
    for nm, src, nr, ncol in WSPEC:
        WB[nm] = (nc.dram_tensor("wbf_" + nm, [nr, ncol], BF16).ap(), [Buf(f"wbf_{nm}_{i}") for i in range(nr // 128)])

    S = Sched(nc)
    NG = NT // GT
    TPG = GT // 128
    XS = 128.0 ** -0.5
    with ExitStack() as st:
        def sb(name, shape, dt, n=1):
            out = []
            for i in range(n):
                t = st.enter_context(nc.sbuf_tensor(f"{name}{i}", shape, dt))
                out.append((t, Buf(f"{name}{i}")))
            return out[0] if n == 1 else Rot(out)

        def ps(name, shape, dt, n=1):
            out = []
            for i in range(n):
                t = st.enter_context(nc.psum_tensor(f"{name}{i}", shape, dt))
                out.append((t, Buf(f"{name}{i}")))
            return out[0] if n == 1 else Rot(out)

        idt, Bid = sb("idt", [128, 128], BF16)
        S.add("sp", lambda e: e.dma_start(out=idt[:], in_=ident), writes=[Bid], dma=True)
        epst, Beps = sb("epst", [128, 1], F32)
        S.add("dve", lambda e: e.memset(epst[:], EPS), writes=[Beps])
        hnt, Bhn = sb("hnt", [128, 2, 768], F32)
        for i in range(2):
            S.add("sp", lambda e, i=i: e.dma_start(out=hnt[:, i, :], in_=hn[i:i + 1, :].partition_broadcast(128)), writes=[Bhn], dma=True, holder=Buf(f"hnl{i}"))
        greps = sb("grep", [128, D], F32, 2)
        xts = [sb(f"xt{i}", [128, D], F32) for i in range(TPG)]
        yts = [sb(f"yt{i}", [128, D], F32) for i in range(TPG)]
        actT, BactT = sb("actT", [128, KC, GT], BF16)
        aT, BaT = sb("aT", [128, DFF // 128, GT], BF16)
        wbufs = sb("wb", [128, KC, 512], BF16, 3)
        junk, Bjunk = sb("junk", [128, D], BF16)
        sss = sb("ss", [128, 8], F32, 3)
        hs = sb("h", [128, D], BF16, 2)
        attt = Rot([sb("attt", [128, 12 * 129], F32)])
        ssum = Rot([sb("ssum", [128, 4 * 129], F32)])
        fbt = Rot([sb("fbt", [128, 2, 768], F32)])
        tsum = Rot([sb("tsum", [128, 768], F32)])
        tsq = Rot([sb("tsq", [128, 768], F32)])
        gtt = Rot([sb("gtt", [128, 1536], BF16)])
        rr = sb("rr", [128, 8], F32, 4)
        KTs, BKT = sb("KTs", [128, 4, MEM], BF16)
        Vs, BV = sb("Vs", [128, 2, 4, 129], BF16)
        qTs, BqT = sb("qTs", [128, 4, GT], BF16)
        Es = sb("E", [128, GT], BF16, 4)
        ots = sb("ot", [128, 512], BF16, 2)
        oT, BoT = sb("oT", [128, 4, GT], BF16)
        relus = sb("relu", [128, GT], F32, 3)
        pTs = ps("pT", [128, 8, 128], BF16, 2)
        pms = ps("pm", [128, 512], F32, 6)

        cast_holders = [Buf(f"casth{i}") for i in range(8)]
        ci = 0
        for nm, src, nr, ncol in WSPEC:
            dst, bl = WB[nm]
            for rc in range(nr // 128):
                S.add("pool", lambda e, dst=dst, src=src, rc=rc: e.dma_start(out=dst[rc * 128:(rc + 1) * 128, :], in_=src[rc * 128:(rc + 1) * 128, :]),
                      writes=[bl[rc]], dma=True, holder=cast_holders[ci % 8])
                ci += 1

        tog = [0]

        def evac_copy(out_ap, in_ap, rb, wb):
            tog[0] ^= 1
            if tog[0]:
                S.add("act", lambda e: e.copy(out=out_ap, in_=in_ap), reads=rb, writes=wb)
            else:
                S.add("dve", lambda e: e.tensor_copy(out=out_ap, in_=in_ap), reads=rb, writes=wb)

        def load_gain(i):
            g, Bg = greps.next()
            S.add("sp", lambda e, g=g, i=i: e.dma_start(out=g[:], in_=gains[i:i + 1, :].partition_broadcast(128)), writes=[Bg], dma=True)
            return g, Bg

        def load_w(nm, r0, nrows, c0, ncols):
            wb, Bw = wbufs.next()
            src, bl = WB[nm]
            S.add("sp", lambda e, wb=wb, src=src: e.dma_start(
                out=wb[:, 0:nrows // 128, 0:ncols], in_=src[r0:r0 + nrows, c0:c0 + ncols].rearrange("(kc p) n -> p kc n", p=128)),
                reads=bl[r0 // 128:(r0 + nrows) // 128], writes=[Bw], dma=True)
            return wb, Bw

        def rstd_of(src_ap, Bsrc, n, scale_div):
            ss, Bss = sss.next()
            S.add("act", lambda e: e.activation(out=junk[:, 0:n], in_=src_ap, func=AF.Square, accum_out=ss[:, 0:1]), reads=[Bsrc], writes=[Bjunk, Bss])
            S.add("act", lambda e: e.activation(out=ss[:, 0:1], in_=ss[:, 0:1], func=AF.Ln, scale=1.0 / scale_div, bias=epst[:, 0:1]), reads=[Bss, Beps], writes=[Bss])
            S.add("act", lambda e: e.activation(out=ss[:, 0:1], in_=ss[:, 0:1], func=AF.Exp, scale=-0.5), reads=[Bss], writes=[Bss])
            return ss, Bss

        def transpose_into(src, Bsrc, nkc, dstT, BdstT, tt):
            for c0 in range(0, nkc, 8):
                nn = min(8, nkc - c0)
                pT, BpT = pTs.next()
                for c in range(nn):
                    S.add("pe", lambda e, pT=pT, c=c, kc=c0 + c: e.transpose(out=pT[:, c, :], in_=src[:, kc * 128:(kc + 1) * 128], identity=idt[:]),
                          reads=[Bsrc, Bid], writes=[BpT])
                evac_copy(dstT[:, c0:c0 + nn, tt * 128:(tt + 1) * 128], pT[:, 0:nn, :], [BpT], [BdstT])

        def norm_T(xt, Bx, g, Bg, tt):
            ss, Bss = rstd_of(xt[:], Bx, D, D)
            h, Bh = hs.next()
            S.add("dve", lambda e: e.scalar_tensor_tensor(out=h[:], in0=xt[:], scalar=ss[:, 0:1], in1=g[:], op0=ALU.mult, op1=ALU.mult),
                  reads=[Bx, Bss, Bg], writes=[Bh])
            transpose_into(h, Bh, KC, actT, BactT, tt)

        def post_res(yt, By, xt, Bx, g, Bg):
            ss, Bss = rstd_of(yt[:], By, D, D)
            S.add("dve", lambda e: e.scalar_tensor_tensor(out=yt[:], in0=yt[:], scalar=ss[:, 0:1], in1=g[:], op0=ALU.mult, op1=ALU.mult),
                  reads=[By, Bss, Bg], writes=[By])
            S.add("dve", lambda e: e.tensor_tensor(out=xt[:], in0=xt[:], in1=yt[:], op=ALU.add), reads=[Bx, By], writes=[Bx])

        def mm_tok(srcT, BsrcT, nkc, wsrc):
            for ch in range(4):
                wb, Bw = load_w(wsrc, 0, nkc * 128, ch * 512, 512)
                for tt in range(TPG):
                    pm, Bpm = pms.next()
                    for kc in range(nkc):
                        S.add("pe", lambda e, pm=pm, wb=wb, kc=kc, tt=tt: e.matmul(pm[:], lhsT=srcT[:, kc, tt * 128:(tt + 1) * 128], rhs=wb[:, kc, :],
                              start=(kc == 0), stop=(kc == nkc - 1)), reads=[Bw, BsrcT], writes=[Bpm])
                    yt, By = yts[tt]
                    evac_copy(yt[:, ch * 512:(ch + 1) * 512], pm[:], [Bpm], [By])

        gm, Bgm = load_gain(5)
        for mt in range(2):
            xt, Bx = xts[mt % TPG]
            S.add("sp", lambda e, xt=xt, mt=mt: e.dma_start(out=xt[:], in_=mem[mt * 128:(mt + 1) * 128, :]), writes=[Bx], dma=True)
            ss, Bss = rstd_of(xt[:], Bx, D, D)
            h, Bh = hs.next()
            S.add("dve", lambda e, h=h, xt=xt, ss=ss: e.scalar_tensor_tensor(out=h[:], in0=xt[:], scalar=ss[:, 0:1], in1=gm[:], op0=ALU.mult, op1=ALU.mult),
                  reads=[Bx, Bss, Bgm], writes=[Bh])
            transpose_into(h, Bh, KC, aT, BaT, mt)
        wb, Bw = load_w("xk", 0, D, 0, 512)
        for hd in range(4):
            pm, Bpm = pms.next()
            for kc in range(KC):
                S.add("pe", lambda e, pm=pm, wb=wb, kc=kc, hd=hd: e.matmul(pm[:, 0:MEM], lhsT=wb[:, kc, hd * 128:(hd + 1) * 128], rhs=aT[:, kc, 0:MEM],
                      start=(kc == 0), stop=(kc == KC - 1)), reads=[Bw, BaT], writes=[Bpm])
            evac_copy(KTs[:, hd, :], pm[:, 0:MEM], [Bpm], [BKT])
        wb, Bw = load_w("xv", 0, D, 0, 512)
        S.add("dve", lambda e: e.memset(Vs[:, :, :, 128:129], 1.0), writes=[BV])
        for mt in range(2):
            pm, Bpm = pms.next()
            for kc in range(KC):
                S.add("pe", lambda e, pm=pm, wb=wb, kc=kc, mt=mt: e.matmul(pm[:], lhsT=aT[:, kc, mt * 128:(mt + 1) * 128], rhs=wb[:, kc, :],
                      start=(kc == 0), stop=(kc == KC - 1)), reads=[Bw, BaT], writes=[Bpm])
            evac_copy(Vs[:, mt, :, 0:128], pm[:].rearrange("p (h e) -> p h e", h=4), [Bpm], [BV])

        def stage1_gen(g):
            t0 = g * GT
            for tt in range(TPG):
                r0 = t0 + tt * 128
                at, Bat = attt.next(); sm, Bsm = ssum.next(); r4, Br4 = rr.next()
                u, Bu = hs.next()
                gt_, Bgt = gtt.next()
                S.add("sp", lambda e, at=at, r0=r0: e.dma_start(out=at[:], in_=att[r0:r0 + 128, :]), writes=[Bat], dma=True)
                S.add("sp", lambda e, gt_=gt_, r0=r0: e.dma_start(out=gt_[:], in_=gate[r0:r0 + 128, :]), writes=[Bgt], dma=True)
                av = at[:].rearrange("p (h q c) -> p h q c", h=4, q=3)
                smv = sm[:].rearrange("p (h c) -> p h c", h=4)
                S.add("dve", lambda e, smv=smv, av=av: e.tensor_tensor(out=smv, in0=av[:, :, 0, :], in1=av[:, :, 1, :], op=ALU.add), reads=[Bat], writes=[Bsm])
                yield
                S.add("dve", lambda e, smv=smv, av=av: e.tensor_tensor(out=smv, in0=smv, in1=av[:, :, 2, :], op=ALU.add), reads=[Bat, Bsm], writes=[Bsm])
                yield
                S.add("dve", lambda e, r4=r4, smv=smv: e.reciprocal(out=r4[:, 0:4], in_=smv[:, :, 128]), reads=[Bsm], writes=[Br4])
                for hd in range(4):
                    S.add("act", lambda e, u=u, smv=smv, r4=r4, hd=hd: e.activation(out=u[:, hd * 128:(hd + 1) * 128], in_=smv[:, hd, 0:128], func=AF.Copy, scale=r4[:, hd:hd + 1]),
                          reads=[Bsm, Br4], writes=[Bu])
                    yield
                for ki, src in enumerate((mfb, gfb)):
                    fb, Bfb = fbt.next(); ts_, Bts = tsum.next(); tq, Btq = tsq.next(); r6, Br6 = rr.next()
                    S.add("sp", lambda e, fb=fb, src=src, r0=r0: e.dma_start(out=fb[:], in_=src[:, r0:r0 + 128, :].rearrange("d p n -> p d n")), writes=[Bfb], dma=True)
                    S.add("dve", lambda e, ts_=ts_, fb=fb: e.tensor_tensor(out=ts_[:], in0=fb[:, 0, :], in1=fb[:, 1, :], op=ALU.add), reads=[Bfb], writes=[Bts])
                    yield
                    S.add("dve", lambda e, tq=tq, ts_=ts_: e.tensor_tensor(out=tq[:], in0=ts_[:], in1=ts_[:], op=ALU.mult), reads=[Bts], writes=[Btq])
                    yield
                    S.add("dve", lambda e, r6=r6, tq=tq: e.tensor_reduce(out=r6[:, 0:6], in_=tq[:].rearrange("p (h e) -> p h e", h=6), axis=AX.X, op=ALU.add),
                          reads=[Btq], writes=[Br6])
                    yield
                    S.add("act", lambda e, r6=r6: e.activation(out=r6[:, 0:6], in_=r6[:, 0:6], func=AF.Ln, scale=1.0 / 128, bias=epst[:, 0:1]), reads=[Br6, Beps], writes=[Br6])
                    yield
                    S.add("act", lambda e, r6=r6: e.activation(out=r6[:, 0:6], in_=r6[:, 0:6], func=AF.Exp, scale=-0.5), reads=[Br6], writes=[Br6])
                    yield
                    for hd in range(6):
                        S.add("act", lambda e, tq=tq, ts_=ts_, r6=r6, hd=hd: e.activation(out=tq[:, hd * 128:(hd + 1) * 128], in_=ts_[:, hd * 128:(hd + 1) * 128], func=AF.Copy, scale=r6[:, hd:hd + 1]),
                              reads=[Bts, Br6], writes=[Btq])
                        yield
                    S.add("dve", lambda e, tq=tq, ki=ki: e.tensor_tensor(out=tq[:], in0=tq[:], in1=hnt[:, ki, :], op=ALU.mult), reads=[Btq, Bhn], writes=[Btq])
                    yield
                    S.add("dve", lambda e, u=u, tq=tq, gt_=gt_, ki=ki: e.tensor_tensor(out=u[:, 512 + ki * 768:512 + (ki + 1) * 768], in0=tq[:], in1=gt_[:, ki * 768:(ki + 1) * 768], op=ALU.mult),
                          reads=[Btq, Bgt], writes=[Bu])
                    yield
                transpose_into(u, Bu, KC, actT, BactT, tt)
            yield
        for _ in stage1_gen(0):
            pass
        for g in range(NG):
            t0 = g * GT
            for tt in range(TPG):
                r0 = t0 + tt * 128
                xt, Bx = xts[tt]
                S.add("sp", lambda e, xt=xt, r0=r0: e.dma_start(out=xt[:], in_=x[r0:r0 + 128, :]), writes=[Bx], dma=True)
            mm_tok(actT, BactT, KC, "out")
            gp, Bgp = load_gain(0)
            for tt in range(TPG):
                post_res(yts[tt][0], yts[tt][1], xts[tt][0], xts[tt][1], gp, Bgp)
            gp, Bgp = load_gain(1)
            for tt in range(TPG):
                norm_T(xts[tt][0], xts[tt][1], gp, Bgp, tt)
            wb, Bw = load_w("xq", 0, D, 0, 512)
            for hd in range(4):
                pm, Bpm = pms.next()
                for kc in range(KC):
                    S.add("pe", lambda e, pm=pm, wb=wb, kc=kc, hd=hd: e.matmul(pm[:, 0:GT], lhsT=wb[:, kc, hd * 128:(hd + 1) * 128], rhs=actT[:, kc, :],
                          start=(kc == 0), stop=(kc == KC - 1)), reads=[Bw, BactT], writes=[Bpm])
                evac_copy(qTs[:, hd, :], pm[:, 0:GT], [Bpm], [BqT])
            otl = [ots.next() for _ in range(TPG)]
            for hd in range(4):
                El = []
                for mt in range(2):
                    pm, Bpm = pms.next()
                    S.add("pe", lambda e, pm=pm, hd=hd, mt=mt: e.matmul(pm[:, 0:GT], lhsT=KTs[:, hd, mt * 128:(mt + 1) * 128], rhs=qTs[:, hd, :], start=True, stop=True),
                          reads=[BKT, BqT], writes=[Bpm])
                    E, BE = Es.next()
                    S.add("act", lambda e, E=E, pm=pm: e.activation(out=E[:], in_=pm[:, 0:GT], func=AF.Exp, scale=XS), reads=[Bpm], writes=[BE])
                    El.append((E, BE))
                for tt in range(TPG):
                    pm, Bpm = pms.next()
                    for mt in range(2):
                        E, BE = El[mt]
                        S.add("pe", lambda e, pm=pm, E=E, tt=tt, mt=mt, hd=hd: e.matmul(pm[:, 0:129], lhsT=E[:, tt * 128:(tt + 1) * 128], rhs=Vs[:, mt, hd, :],
                              start=(mt == 0), stop=(mt == 1)), reads=[BE, BV], writes=[Bpm])
                    r1, Br1 = rr.next()
                    ot, Bot = otl[tt]
                    S.add("dve", lambda e, r1=r1, pm=pm: e.reciprocal(out=r1[:, 0:1], in_=pm[:, 128:129]), reads=[Bpm], writes=[Br1])
                    S.add("act", lambda e, ot=ot, pm=pm, r1=r1, hd=hd: e.activation(out=ot[:, hd * 128:(hd + 1) * 128], in_=pm[:, 0:128], func=AF.Copy, scale=r1[:, 0:1]),
                          reads=[Bpm, Br1], writes=[Bot])
            for tt in range(TPG):
                transpose_into(otl[tt][0], otl[tt][1], 4, oT, BoT, tt)
            mm_tok(oT, BoT, 4, "xo")
            gp, Bgp = load_gain(2)
            for tt in range(TPG):
                post_res(yts[tt][0], yts[tt][1], xts[tt][0], xts[tt][1], gp, Bgp)
            gp, Bgp = load_gain(3)
            for tt in range(TPG):
                norm_T(xts[tt][0], xts[tt][1], gp, Bgp, tt)
            for fc in range(DFF // 512):
                wb, Bw = load_w("up", 0, D, fc * 512, 512)
                for fj in range(4):
                    f = fc * 4 + fj
                    pm, Bpm = pms.next()
                    for kc in range(KC):
                        S.add("pe", lambda e, pm=pm, wb=wb, kc=kc, fj=fj: e.matmul(pm[:, 0:GT], lhsT=wb[:, kc, fj * 128:(fj + 1) * 128], rhs=actT[:, kc, :],
                              start=(kc == 0), stop=(kc == KC - 1)), reads=[Bw, BactT], writes=[Bpm])
                    rl, Brl = relus.next()
                    S.add("act", lambda e, rl=rl, pm=pm: e.activation(out=rl[:], in_=pm[:, 0:GT], func=AF.Relu), reads=[Bpm], writes=[Brl])
                    S.add("dve", lambda e, rl=rl, f=f: e.tensor_tensor(out=aT[:, f, :], in0=rl[:], in1=rl[:], op=ALU.mult), reads=[Brl], writes=[BaT])
            nxt = stage1_gen(g + 1) if g + 1 < NG else iter(())
            for dq in range(4):
                pol = [pms.next() for _ in range(TPG)]
                for f8 in range(DFF // 1024):
                    for _ in range(3):
                        next(nxt, None)
                    wb, Bw = load_w("down", f8 * 1024, 1024, dq * 512, 512)
                    for fi in range(8):
                        f = f8 * 8 + fi
                        for tt in range(TPG):
                            pm, Bpm = pol[tt]
                            S.add("pe", lambda e, pm=pm, wb=wb, fi=fi, f=f, tt=tt: e.matmul(pm[:], lhsT=aT[:, f, tt * 128:(tt + 1) * 128], rhs=wb[:, fi, :],
                                  start=(f == 0), stop=(f == DFF // 128 - 1)), reads=[Bw, BaT], writes=[Bpm])
                for tt in range(TPG):
                    pm, Bpm = pol[tt]
                    yt, By = yts[tt]
                    evac_copy(yt[:, dq * 512:(dq + 1) * 512], pm[:], [Bpm], [By])
            for _ in nxt:
                pass
            gp, Bgp = load_gain(4)
            for tt in range(TPG):
                post_res(yts[tt][0], yts[tt][1], xts[tt][0], xts[tt][1], gp, Bgp)
                r0 = t0 + tt * 128
                S.add("sp", lambda e, xt=xts[tt][0], r0=r0: e.dma_start(out=xo[r0:r0 + 128, :], in_=xt[:]), reads=[xts[tt][1]], dma=True)
        S.emit(st)
    return nc, S

OFF = {}
_names = ["aq","ak","av","mq","mk","mv","mo","mi","mf","gq","gk","gv","gg","gz"]
_sizes = [512,512,512,384,384,768,768,12,12,384,384,768,768,32]
_o = 0
for _n, _s in zip(_names, _sizes):
    OFF[_n] = (_o, _o + _s); _o += _s
def _r(n): return np.arange(*OFF[n])
FM_COLS = np.concatenate([_r("aq"), _r("ak")] +
    [np.concatenate([_r("mq")[h*64:(h+1)*64], _r("mk")[h*64:(h+1)*64]]) for h in range(6)] +
    [np.concatenate([_r("gq")[h*64:(h+1)*64], _r("gk")[h*64:(h+1)*64]]) for h in range(6)])
TM_COLS = np.concatenate([_r("av"), _r("mv"), _r("gv"), _r("mo"), _r("gg"), _r("mi"), _r("mf")])
GZ_COLS = _r("gz")
SW_COLS = np.concatenate([np.concatenate([_r(n)[h*128+64:(h+1)*128], _r(n)[h*128:h*128+64]]) for n in ("aq", "ak") for h in range(4)])

def rope_tables(pos):
    half = 64
    inv = (np.float32(10000.0) ** (-np.arange(half, dtype=np.float32) / np.float32(half))).astype(np.float32)
    ang = pos.astype(np.float32)[:, None] * inv[None, :]
    cos = np.cos(ang).astype(np.float32).T
    sin = np.sin(ang).astype(np.float32).T
    cosT = np.concatenate([cos, cos], 0)
    sinT = np.concatenate([-sin, sin], 0)
    return np.ascontiguousarray(cosT), np.ascontiguousarray(sinT)

def a_inputs(l, x_shard, pos, P):
    w_in = P["w_in"][l]
    cosT, sinT = rope_tables(pos)
    wdec = np.zeros((33, 768), np.float32)
    wdec[0:16, 0:384] = P["gla_w_decay"][l, 0]
    wdec[16:32, 384:768] = P["gla_w_decay"][l, 1]
    wdec[32, 0:384] = P["gla_b_decay"][l, 0]
    wdec[32, 384:768] = P["gla_b_decay"][l, 1]
    ifb = np.concatenate([P["mlstm_i_bias"][l].reshape(-1), P["mlstm_f_bias"][l].reshape(-1)])[None, :]
    return dict(x=np.ascontiguousarray(x_shard), g_pre=P["norm_mix_pre"][l][None, :],
                w_fm=np.ascontiguousarray(w_in[:, FM_COLS]), w_gz=np.ascontiguousarray(w_in[:, GZ_COLS]), w_sw=np.ascontiguousarray(w_in[:, SW_COLS]),
                w_tm=np.ascontiguousarray(w_in[:, TM_COLS]), ifb=np.ascontiguousarray(ifb.astype(np.float32)),
                wdec=wdec, cosT=cosT, sinT=sinT, ident=np.eye(128).astype(ml_dtypes.bfloat16))


BF = ml_dtypes.bfloat16
SEQ = 16384
NCORE = 8
NTC = SEQ // 4


def _run(nc, in_maps):
    res = run_bass_kernel_spmd(nc, in_maps, core_ids=list(range(len(in_maps))))
    return res.results


def _att_mask():
    a = np.arange(128)[:, None]; i = np.arange(128)[None, :]
    return np.concatenate([(a <= i - 64), (np.abs(a - i) <= 64), (a >= i + 64)], 1).astype(BF)


def _perm_tok(x, d):
    n = x.shape[0] // d
    return np.ascontiguousarray(x.reshape(n, d, *x.shape[1:]).swapaxes(0, 1).reshape(x.shape))


def _unperm_tok(x, d):
    n = x.shape[0] // d
    return np.ascontiguousarray(x.reshape(d, n, *x.shape[1:]).swapaxes(0, 1).reshape(x.shape))


def _gather_batch(A_out, b):
    FM = np.concatenate([A_out[b * 4 + s]["fm"] for s in range(4)], axis=1)
    TM = np.concatenate([A_out[b * 4 + s]["tm"] for s in range(4)], axis=0)
    GA = np.concatenate([A_out[b * 4 + s]["gates"] for s in range(4)], axis=0)
    GS = np.concatenate([A_out[b * 4 + s]["gsp"] for s in range(4)], axis=0)
    return FM, TM, GA, GS


def _b1_inputs(batches):
    in_maps = []
    mask = _att_mask()
    for c in range(NCORE):
        b, k = c // 4, c % 4
        FM, TM, GA, GS = batches[b]
        qT = FM[k * 128:(k + 1) * 128]; kT = FM[512 + k * 128:512 + (k + 1) * 128]; v = TM[:, k * 128:(k + 1) * 128]
        in_maps.append(dict(
            qT=np.stack([_perm_tok(np.ascontiguousarray(qT.T), d).T for d in DILS]),
            kT=np.stack([_perm_tok(np.ascontiguousarray(kT.T), d).T for d in DILS]),
            v=np.stack([_perm_tok(np.ascontiguousarray(v), d) for d in DILS]), mask=mask))
        in_maps[-1]["qT"] = np.ascontiguousarray(in_maps[-1]["qT"]); in_maps[-1]["kT"] = np.ascontiguousarray(in_maps[-1]["kT"])
    return in_maps


def _b2_inputs(l, P, batches):
    in_maps = []
    NCH = SEQ // 128
    for c in range(NCORE):
        b, k = c // 4, c % 4
        FM, TM, GA, GS = batches[b]
        mqk = np.zeros((3, 2, 64, SEQ + 4), BF); cw = np.zeros((3, 64, 10), np.float32); cb = np.zeros((3, 64, 2), np.float32)
        nlf = np.zeros((3, 128, NCH), np.float32); ei = np.zeros((3, 128, NCH), np.float32)
        mv = np.zeros((3, SEQ, 128), BF); gqk = np.zeros((3, 2, 64, SEQ), BF)
        gsp = np.zeros((3, SEQ, 64), np.float32); gv = np.zeros((3, SEQ, 128), BF)
        for u in range(3):
            idx = 3 * k + u
            h, d = idx // 2, idx % 2
            fl = (lambda a, ax: np.flip(a, axis=ax)) if d else (lambda a, ax: a)
            r0 = 1024 + h * 128
            mqk[u, 0, :, 2:-2] = fl(FM[r0:r0 + 64], 1); mqk[u, 1, :, 2:-2] = fl(FM[r0 + 64:r0 + 128], 1)
            cwl = P["mlstm_conv_w"][l]
            cw[u, :, 0:5] = fl(cwl[:, h * 64:(h + 1) * 64], 0).T; cw[u, :, 5:10] = fl(cwl[:, 384 + h * 64:384 + (h + 1) * 64], 0).T
            cb[u, :, 0] = P["mlstm_conv_b"][l][h * 64:(h + 1) * 64]; cb[u, :, 1] = P["mlstm_conv_b"][l][384 + h * 64:384 + (h + 1) * 64]
            nlf[u] = fl(GA[:, d * 6 + h], 0).reshape(NCH, 128).T
            ei[u] = fl(GA[:, 12 + d * 6 + h], 0).reshape(NCH, 128).T
            mv[u] = fl(TM[:, 512 + h * 128:512 + (h + 1) * 128], 0)
            r0 = 1792 + h * 128
            gqk[u, 0] = fl(FM[r0:r0 + 64], 1); gqk[u, 1] = fl(FM[r0 + 64:r0 + 128], 1)
            gsp[u] = fl(GS[:, d * 384 + h * 64:d * 384 + (h + 1) * 64], 0)
            gv[u] = fl(TM[:, 1280 + h * 128:1280 + (h + 1) * 128], 0)
        in_maps.append(dict(mqk=mqk, cw=cw, cb=cb, nlf=nlf, ei=ei, mv=mv, gqk=gqk, gsp=gsp, gv=gv,
                            triU=np.triu(np.ones((128, 128), np.float32)), ident=np.eye(128).astype(BF)))
    return in_maps


def _c_inputs(l, P, x, batches, B1_out, B2_out):
    gains = np.ascontiguousarray(np.stack([P["norm_mix_post"][l], P["norm_xatt_pre"][l], P["norm_xatt_post"][l],
                                           P["norm_mlp_pre"][l], P["norm_mlp_post"][l], P["norm_mem"][l]]))
    hn = np.ascontiguousarray(np.stack([P["mlstm_norm"][l], P["gla_norm"][l]]))
    ident = np.eye(128).astype(BF)
    per_b = []
    for b in range(2):
        att = np.zeros((SEQ, 4, 3, 129), np.float32)
        mfb = np.zeros((2, SEQ, 768), np.float32); gfb = np.zeros((2, SEQ, 768), np.float32)
        for k in range(4):
            c = b * 4 + k
            for p, d in enumerate(DILS):
                att[:, k, p, :] = _unperm_tok(B1_out[c]["o"][p], d)
            for u in range(3):
                idx = 3 * k + u
                h, d = idx // 2, idx % 2
                ym = B2_out[c]["ym"][u]; yg = B2_out[c]["yg"][u]
                if d:
                    ym = ym[::-1]; yg = yg[::-1]
                mfb[d, :, h * 128:(h + 1) * 128] = ym
                gfb[d, :, h * 128:(h + 1) * 128] = yg
        per_b.append((att.reshape(SEQ, 12 * 129), mfb, gfb))
    in_maps = []
    for c in range(NCORE):
        b, s = c // 4, c % 4
        att, mfb, gfb = per_b[b]
        sl = slice(s * NTC, (s + 1) * NTC)
        TM = batches[b][1]
        in_maps.append(dict(x=np.ascontiguousarray(x[b, sl]), att=np.ascontiguousarray(att[sl]), mfb=np.ascontiguousarray(mfb[:, sl]),
                            gfb=np.ascontiguousarray(gfb[:, sl]), gate=np.ascontiguousarray(TM[sl, 2048:3584]), hn=hn, gains=gains,
                            mem=np.ascontiguousarray(P["mem"][b]), w_out=P["w_out"][l], w_xq=P["w_xq"][l], w_xk=P["w_xk"][l],
                            w_xv=P["w_xv"][l], w_xo=P["w_xo"][l], w_up=P["w_up"][l], w_down=P["w_down"][l], ident=ident))
    return in_maps


def kernel(**inputs):
    P = {k: np.asarray(v) for k, v in inputs.items()}
    x = np.asarray(P["x"], dtype=np.float32)
    ncA, _ = build_A(NTC, 1024)
    ncB1, _ = build_B1(SEQ)
    ncB2, _ = build_B2(SEQ, 3)
    ncC, _ = build_C(NTC, 256)
    for l in range(2):
        a_maps = []
        for c in range(NCORE):
            b, s = c // 4, c % 4
            a_maps.append(a_inputs(l, x[b, s * NTC:(s + 1) * NTC], np.arange(s * NTC, (s + 1) * NTC), P))
        A_out = _run(ncA, a_maps)
        batches = [_gather_batch(A_out, b) for b in range(2)]
        del A_out
        B1_out = _run(ncB1, _b1_inputs(batches))
        B2_out = _run(ncB2, _b2_inputs(l, P, batches))
        C_out = _run(ncC, _c_inputs(l, P, x, batches, B1_out, B2_out))
        x = np.stack([np.concatenate([C_out[b * 4 + s]["xo"] for s in range(4)], axis=0) for b in range(2)]).astype(np.float32)
    return x
```
